# Optimizing a Trainium2 kernel written in Bass

```python
import math
import jax
import jax.numpy as jnp
from jax import lax
import numpy as np

D_MODEL = 1024
BATCH = 16
SEQ = 4096
DEPTH = 2

GRID_W = 64
CTX_LEN = 256
N_EVEN = (DEPTH + 1) // 2
N_ODD = DEPTH // 2
D_CONV = D_MODEL // 2
CONV_WIDTH = 31
D_SSM = D_MODEL // 2
SSM_GROUP = 16
N_SSM_GROUPS = D_SSM // SSM_GROUP
SSM_STATE = 64
DT_MIN = 1e-3
DT_MAX = 1e-1
D_IN_AB = 2 * D_CONV + D_SSM
HEAD_DIM = 128
N_HEADS = D_MODEL // HEAD_DIM
N_KV_HEADS = N_HEADS // 4
AXIS_ROPE_DIM = HEAD_DIM // 2
ROPE_THETA = 10000.0
Q_BLOCK = 128
D_Q = N_HEADS * HEAD_DIM
D_KV = N_KV_HEADS * HEAD_DIM
D_IN_C = D_Q + 2 * D_KV
D_FF = int(math.ceil(8 * D_MODEL / 3 / 256)) * 256
ALPHA = (2 * DEPTH) ** 0.25
OUT_SCALE = (8 * DEPTH) ** -0.25
LN_EPS = 1e-5
RMS_EPS = 1e-6

kernel_name = "hybrid_conv_s5_gqa_dit_block"


def layer_norm(x, g, b):
    x32 = x.astype(jnp.float32)
    mu = jnp.mean(x32, axis=-1, keepdims=True)
    var = jnp.mean(jnp.square(x32 - mu), axis=-1, keepdims=True)
    return ((x32 - mu) * lax.rsqrt(var + LN_EPS)).astype(x.dtype) * g + b


def rms_norm(x, g):
    x32 = x.astype(jnp.float32)
    y = x32 * lax.rsqrt(jnp.mean(jnp.square(x32), axis=-1, keepdims=True) + RMS_EPS)
    return y.astype(x.dtype) * g


def modulate(x, shift, scale):
    return x * (1 + scale) + shift


def swiglu(h, w_in, w_out):
    gate, up = jnp.split(h @ w_in, 2, axis=-1)
    return (jax.nn.silu(gate) * up) @ w_out


def conformer_conv(a_in, conv_w, conv_b, ln_g, ln_b):
    val, gate = jnp.split(a_in, 2, axis=-1)
    a = val * jax.nn.sigmoid(gate)
    pad = CONV_WIDTH // 2
    a = lax.conv_general_dilated(a, conv_w[:, None, :], window_strides=(1,), padding=[(pad, pad)],
                                 dimension_numbers=("NWC", "WIO", "NWC"),
                                 feature_group_count=D_CONV) + conv_b
    return jax.nn.silu(layer_norm(a, ln_g, ln_b))


def _ssm_combine(e1, e2):
    a1, b1 = e1
    a2, b2 = e2
    return a1 * a2, a2 * b1 + b2


def ssm_discretize(lam_re, lam_im, log_dt, b_re, b_im, c_re, c_im):
    lam = lax.complex(lam_re.astype(jnp.float32), lam_im.astype(jnp.float32))
    dt = jnp.exp(log_dt.astype(jnp.float32))[..., None]
    lam_bar = jnp.exp(lam * dt)
    b_mat = lax.complex(b_re.astype(jnp.float32), b_im.astype(jnp.float32))
    b_bar = ((lam_bar - 1.0) / lam)[..., None] * b_mat
    c_mat = lax.complex(c_re.astype(jnp.float32), c_im.astype(jnp.float32))
    return lam_bar, b_bar, c_mat


def ssm_scan(u, lam_bar, b_bar, s0, reverse):
    b, n, _ = u.shape
    ug = u.astype(jnp.float32).reshape(b, n, N_SSM_GROUPS, SSM_GROUP).astype(jnp.complex64)
    bu = jnp.einsum("blgh,gph->blgp", ug, b_bar)
    first = n - 1 if reverse else 0
    bu = bu.at[:, first].add(lam_bar * s0)
    a = jnp.broadcast_to(lam_bar, bu.shape)
    _, s = lax.associative_scan(_ssm_combine, (a, bu), reverse=reverse, axis=1)
    return s


def ssm_readout(u, s_fwd, s_bwd, c_mat, d_skip, w_glu, b_glu):
    b, n, _ = u.shape
    y = jnp.einsum("blgp,ghp->blgh", s_fwd, c_mat[0]) + jnp.einsum("blgp,ghp->blgh", s_bwd, c_mat[1])
    y = jnp.real(y).reshape(b, n, D_SSM) + d_skip.astype(jnp.float32) * u.astype(jnp.float32)
    y = jax.nn.gelu(y.astype(u.dtype))
    return y * jax.nn.sigmoid(y @ w_glu + b_glu)


def conv_ssm_mixer(h_lat, h_ctx, need_ctx, w_in, conv_w, conv_b, conv_ln_g, conv_ln_b,
                   lam_re, lam_im, log_dt, b_re, b_im, c_re, c_im, d_skip, w_glu, b_glu, w_out):
    lam_bar, b_bar, c_mat = ssm_discretize(lam_re, lam_im, log_dt, b_re, b_im, c_re, c_im)
    p_lat = h_lat @ w_in
    p_ctx = h_ctx @ w_in
    u_lat = p_lat[..., 2 * D_CONV:]
    u_ctx = p_ctx[..., 2 * D_CONV:]
    zero = jnp.zeros((h_ctx.shape[0], N_SSM_GROUPS, SSM_STATE), jnp.complex64)
    s_ctx_f = ssm_scan(u_ctx, lam_bar[0], b_bar[0], zero, False)
    s_ctx_b = ssm_scan(u_ctx, lam_bar[1], b_bar[1], zero, True)
    s_lat_f = ssm_scan(u_lat, lam_bar[0], b_bar[0], s_ctx_f[:, -1], False)
    s_lat_b = ssm_scan(u_lat, lam_bar[1], b_bar[1], s_ctx_b[:, 0], True)
    y_lat = jnp.concatenate(
        [conformer_conv(p_lat[..., :2 * D_CONV], conv_w, conv_b, conv_ln_g, conv_ln_b),
         ssm_readout(u_lat, s_lat_f, s_lat_b, c_mat, d_skip, w_glu, b_glu)], axis=-1) @ w_out
    y_ctx = None
    if need_ctx:
        y_ctx = jnp.concatenate(
            [conformer_conv(p_ctx[..., :2 * D_CONV], conv_w, conv_b, conv_ln_g, conv_ln_b),
             ssm_readout(u_ctx, s_ctx_f, s_ctx_b, c_mat, d_skip, w_glu, b_glu)], axis=-1) @ w_out
    return y_lat, y_ctx


def axial_rope_tables(n_tokens):
    rows = n_tokens // GRID_W
    row_idx, col_idx = jnp.meshgrid(jnp.arange(rows, dtype=jnp.float32),
                                    jnp.arange(GRID_W, dtype=jnp.float32), indexing="ij")
    freqs = ROPE_THETA ** (-jnp.arange(0, AXIS_ROPE_DIM, 2, dtype=jnp.float32) / AXIS_ROPE_DIM)
    ang = jnp.concatenate([row_idx.reshape(-1, 1) * freqs, col_idx.reshape(-1, 1) * freqs], axis=-1)
    return jnp.cos(ang), jnp.sin(ang)


def apply_rope(x, cos, sin):
    x32 = x.astype(jnp.float32)
    x1, x2 = jnp.split(x32, 2, axis=-1)
    c = cos[None, :, None, :]
    s = sin[None, :, None, :]
    return jnp.concatenate([x1 * c - x2 * s, x1 * s + x2 * c], axis=-1).astype(x.dtype)


def block_attention(q, k, v):
    b, lq, _, hd = q.shape
    grp = N_HEADS // N_KV_HEADS
    n_blk = lq // Q_BLOCK
    qb = q.reshape(b, n_blk, Q_BLOCK, N_KV_HEADS, grp, hd).transpose(1, 0, 2, 3, 4, 5)
    scale = hd ** -0.5

    def one_block(q_blk):
        s = jnp.einsum("bqkgd,bskd->bkgqs", q_blk, k).astype(jnp.float32) * scale
        p = jax.nn.softmax(s, axis=-1).astype(v.dtype)
        return jnp.einsum("bkgqs,bskd->bqkgd", p, v)

    o = lax.map(one_block, qb)
    return o.transpose(1, 0, 2, 3, 4, 5).reshape(b, lq, N_HEADS * hd)


def attention_mixer(h_lat, h_ctx, need_ctx, cos, sin, w_in, q_norm_g, k_norm_g, w_out):
    b, n, _ = h_lat.shape
    m = h_ctx.shape[1]
    q, k, v = jnp.split(h_lat @ w_in, [D_Q, D_Q + D_KV], axis=-1)
    q = apply_rope(rms_norm(q.reshape(b, n, N_HEADS, HEAD_DIM), q_norm_g), cos, sin)
    k = apply_rope(rms_norm(k.reshape(b, n, N_KV_HEADS, HEAD_DIM), k_norm_g), cos, sin)
    v = v.reshape(b, n, N_KV_HEADS, HEAD_DIM)
    k_ctx, v_ctx = jnp.split(h_ctx @ w_in[:, D_Q:], 2, axis=-1)
    k_ctx = rms_norm(k_ctx.reshape(b, m, N_KV_HEADS, HEAD_DIM), k_norm_g)
    v_ctx = v_ctx.reshape(b, m, N_KV_HEADS, HEAD_DIM)
    o_lat = block_attention(q, jnp.concatenate([k, k_ctx], axis=1), jnp.concatenate([v, v_ctx], axis=1))
    y_lat = o_lat @ w_out
    y_ctx = None
    if need_ctx:
        q_ctx = rms_norm((h_ctx @ w_in[:, :D_Q]).reshape(b, m, N_HEADS, HEAD_DIM), q_norm_g)
        y_ctx = block_attention(q_ctx, k_ctx, v_ctx) @ w_out
    return y_lat, y_ctx


def setup_inputs(seed: int = 0) -> dict:
    key = jax.random.key(seed)
    ks = jax.random.split(key, 32)
    f32 = jnp.float32

    def nrm(k, shape, scale=1.0):
        return scale * jax.random.normal(k, shape, f32)

    gs = (N_EVEN, 2, N_SSM_GROUPS, SSM_STATE)
    n_idx = jnp.arange(SSM_STATE, dtype=f32)
    return {
        "x": nrm(ks[0], (BATCH, SEQ, D_MODEL)),
        "c": nrm(ks[1], (BATCH, D_MODEL)),
        "ctx": nrm(ks[2], (BATCH, CTX_LEN, D_MODEL)),
        "c_ctx": nrm(ks[3], (D_MODEL,)),
        "w_mod": nrm(ks[4], (DEPTH, D_MODEL, 6 * D_MODEL), D_MODEL ** -0.5),
        "b_mod": nrm(ks[5], (DEPTH, 6 * D_MODEL), 0.02),
        "ln_g": 1.0 + nrm(ks[6], (DEPTH, 2, D_MODEL), 0.05),
        "ln_b": nrm(ks[7], (DEPTH, 2, D_MODEL), 0.02),
        "w_in_ab": nrm(ks[8], (N_EVEN, D_MODEL, D_IN_AB), D_MODEL ** -0.5),
        "conv_w": nrm(ks[9], (N_EVEN, CONV_WIDTH, D_CONV), CONV_WIDTH ** -0.5),
        "conv_b": nrm(ks[10], (N_EVEN, D_CONV), 0.02),
        "conv_ln_g": 1.0 + nrm(ks[11], (N_EVEN, D_CONV), 0.05),
        "conv_ln_b": nrm(ks[12], (N_EVEN, D_CONV), 0.02),
        "ssm_lambda_re": -0.5 + nrm(ks[13], gs, 0.02),
        "ssm_lambda_im": jnp.pi * n_idx + nrm(ks[14], gs, 0.02),
        "ssm_log_dt": jax.random.uniform(ks[15], (N_EVEN, 2, N_SSM_GROUPS), f32,
                                         math.log(DT_MIN), math.log(DT_MAX)),
        "ssm_b_re": nrm(ks[16], (N_EVEN, 2, N_SSM_GROUPS, SSM_STATE, SSM_GROUP), (2 * SSM_GROUP) ** -0.5),
        "ssm_b_im": nrm(ks[17], (N_EVEN, 2, N_SSM_GROUPS, SSM_STATE, SSM_GROUP), (2 * SSM_GROUP) ** -0.5),
        "ssm_c_re": nrm(ks[18], (N_EVEN, 2, N_SSM_GROUPS, SSM_GROUP, SSM_STATE), (2 * SSM_STATE) ** -0.5),
        "ssm_c_im": nrm(ks[19], (N_EVEN, 2, N_SSM_GROUPS, SSM_GROUP, SSM_STATE), (2 * SSM_STATE) ** -0.5),
        "ssm_d": nrm(ks[20], (N_EVEN, D_SSM)),
        "ssm_w_glu": nrm(ks[21], (N_EVEN, D_SSM, D_SSM), D_SSM ** -0.5),
        "ssm_b_glu": nrm(ks[22], (N_EVEN, D_SSM), 0.02),
        "w_out_ab": nrm(ks[23], (N_EVEN, D_CONV + D_SSM, D_MODEL), (D_CONV + D_SSM) ** -0.5 * OUT_SCALE),
        "w_in_c": nrm(ks[24], (N_ODD, D_MODEL, D_IN_C), D_MODEL ** -0.5),
        "q_norm_g": 1.0 + nrm(ks[25], (N_ODD, HEAD_DIM), 0.05),
        "k_norm_g": 1.0 + nrm(ks[26], (N_ODD, HEAD_DIM), 0.05),
        "w_out_c": nrm(ks[27], (N_ODD, D_Q, D_MODEL), D_Q ** -0.5 * OUT_SCALE),
        "w_ffn_in": nrm(ks[28], (DEPTH, D_MODEL, 2 * D_FF), D_MODEL ** -0.5),
        "w_ffn_out": nrm(ks[29], (DEPTH, D_FF, D_MODEL), D_FF ** -0.5 * OUT_SCALE),
    }


def reference(x, c, ctx, c_ctx, w_mod, b_mod, ln_g, ln_b, w_in_ab, conv_w, conv_b, conv_ln_g, conv_ln_b,
              ssm_lambda_re, ssm_lambda_im, ssm_log_dt, ssm_b_re, ssm_b_im, ssm_c_re, ssm_c_im,
              ssm_d, ssm_w_glu, ssm_b_glu, w_out_ab, w_in_c, q_norm_g, k_norm_g, w_out_c,
              w_ffn_in, w_ffn_out):
    cos, sin = axial_rope_tables(x.shape[1])
    xc = ctx
    for l in range(DEPTH):
        need_ctx = l < DEPTH - 1
        mod_lat = (jax.nn.silu(c) @ w_mod[l] + b_mod[l])[:, None, :]
        mod_ctx = jax.nn.silu(c_ctx) @ w_mod[l] + b_mod[l]
        sh_m, sc_m, g_m, sh_f, sc_f, g_f = jnp.split(mod_lat, 6, axis=-1)
        csh_m, csc_m, cg_m, csh_f, csc_f, cg_f = jnp.split(mod_ctx, 6, axis=-1)
        h_lat = modulate(x, sh_m, sc_m)
        h_ctx = modulate(xc, csh_m, csc_m)
        if l % 2 == 0:
            e = l // 2
            y_lat, y_ctx = conv_ssm_mixer(h_lat, h_ctx, need_ctx, w_in_ab[e], conv_w[e], conv_b[e],
                                          conv_ln_g[e], conv_ln_b[e], ssm_lambda_re[e], ssm_lambda_im[e],
                                          ssm_log_dt[e], ssm_b_re[e], ssm_b_im[e], ssm_c_re[e], ssm_c_im[e],
                                          ssm_d[e], ssm_w_glu[e], ssm_b_glu[e], w_out_ab[e])
        else:
            o = l // 2
            y_lat, y_ctx = attention_mixer(h_lat, h_ctx, need_ctx, cos, sin, w_in_c[o],
                                           q_norm_g[o], k_norm_g[o], w_out_c[o])
        x = layer_norm(ALPHA * x + g_m * y_lat, ln_g[l, 0], ln_b[l, 0])
        x = layer_norm(ALPHA * x + g_f * swiglu(modulate(x, sh_f, sc_f), w_ffn_in[l], w_ffn_out[l]),
                       ln_g[l, 1], ln_b[l, 1])
        if need_ctx:
            xc = layer_norm(ALPHA * xc + cg_m * y_ctx, ln_g[l, 0], ln_b[l, 0])
            xc = layer_norm(ALPHA * xc + cg_f * swiglu(modulate(xc, csh_f, csc_f), w_ffn_in[l], w_ffn_out[l]),
                            ln_g[l, 1], ln_b[l, 1])
    return x
```

```python
import math
from contextlib import ExitStack
import numpy as np
import concourse.bass as bass
import concourse.mybir as mybir
from concourse.bass_utils import run_bass_kernel_spmd

F32 = mybir.dt.float32
BF16 = mybir.dt.bfloat16
I32 = mybir.dt.int32
AF = mybir.ActivationFunctionType
ALU = mybir.AluOpType
AX = mybir.AxisListType

D = 1024
SEQ = 4096
CTX = 256
NTOK = SEQ + CTX
DFF = 2816
ALPHA = 4.0 ** 0.25
LN_EPS = 1e-5
RMS_EPS = 1e-6
NSEQ = 2

ENGS = ("pe", "act", "dve", "pool", "sp")
NSEM = 8


class Op:
    __slots__ = ("eng", "fn", "deps", "flag", "count", "stream", "scount", "sidx")

    def __init__(self, eng, fn, stream):
        self.eng = eng
        self.fn = fn
        self.deps = ()
        self.flag = False
        self.count = 0
        self.stream = stream
        self.scount = 0
        self.sidx = 0


class Sched:
    def __init__(self, nc):
        self.nc = nc
        self.q = {e: [] for e in ENGS}
        self.last_w = {}
        self.readers = {}
        self.streams = {}
        self.last_stream_op = {}

    def add(self, eng, fn, r=(), w=(), stream=None, extra=()):
        op = Op(eng, fn, stream)
        deps = set(extra)
        for t in r:
            lw = self.last_w.get(t)
            if lw is not None:
                deps.add(lw)
        for t in w:
            lw = self.last_w.get(t)
            if lw is not None:
                deps.add(lw)
            for rd in self.readers.get(t, ()):
                deps.add(rd)
        for t in r:
            self.readers.setdefault(t, []).append(op)
        for t in w:
            self.last_w[t] = op
            self.readers[t] = []
        deps.discard(op)
        if stream is not None:
            n = self.streams.get(stream, 0)
            self.streams[stream] = n + 1
            op.sidx = n % NSEM
            op.scount = n // NSEM + 1
            self.last_stream_op[(stream, op.sidx)] = op
        op.deps = deps
        self.q[eng].append(op)
        return op

    def pe(self, fn, r=(), w=()):
        return self.add("pe", fn, r, w)

    def act(self, fn, r=(), w=()):
        return self.add("act", fn, r, w)

    def dve(self, fn, r=(), w=()):
        return self.add("dve", fn, r, w)

    def pool(self, fn, r=(), w=()):
        return self.add("pool", fn, r, w)

    def dma(self, eng, out, in_, r=(), w=(), stream="ld", **kw):
        def fn(e, out=out, in_=in_, kw=kw):
            return e.dma_start(out=out, in_=in_, **kw)
        return self.add(eng, fn, r, w, stream=stream)

    def barrier(self):
        lasts = []
        for e in ("pe", "act", "dve", "pool"):
            for o in reversed(self.q[e]):
                if o.fn is not None and o.stream is None:
                    lasts.append(o)
                    break
        lasts += list(self.last_stream_op.values())
        for e in ENGS:
            self.add(e, None, extra=lasts)
        self.last_w = {}
        self.readers = {}

    def emit(self, final_waits=()):
        nc = self.nc
        for e in ENGS:
            for op in self.q[e]:
                for p in op.deps:
                    if p.stream is not None:
                        continue
                    if p.eng == "pe" and op.eng == "pe" and op.stream is None and op.fn is not None:
                        continue
                    p.flag = True
        for e in ENGS:
            c = 0
            for op in self.q[e]:
                if op.stream is None and op.flag:
                    c += 1
                    op.count = c
        sems = {e: nc.alloc_semaphore("sem_" + e) for e in ("pe", "act", "dve", "pool")}
        ssem = {(s, i): nc.alloc_semaphore("dma_%s%d" % (s, i)) for s in self.streams for i in range(NSEM)}
        engmap = {"pe": "tensor", "act": "scalar", "dve": "vector", "pool": "gpsimd", "sp": "sync"}
        with nc.Block() as block:
            for e in ENGS:
                ops = self.q[e]

                def body(eng, ops=ops, e=e):
                    waited = {}
                    for op in ops:
                        need = {}
                        for p in op.deps:
                            if p.stream is not None:
                                key = ("s", p.stream, p.sidx)
                                val = 16 * p.scount
                            else:
                                if (p.eng == "pe" and e == "pe" and op.stream is None
                                        and op.fn is not None):
                                    continue
                                key = ("e", p.eng)
                                val = p.count
                            if val > need.get(key, 0):
                                need[key] = val
                        if op.stream is not None and op.scount > 1:
                            key = ("s", op.stream, op.sidx)
                            val = 16 * (op.scount - 1)
                            if val > need.get(key, 0):
                                need[key] = val
                        for key, val in need.items():
                            if waited.get(key, 0) >= val:
                                continue
                            waited[key] = val
                            sem = ssem[key[1:]] if key[0] == "s" else sems[key[1]]
                            eng.wait_ge(sem, val)
                        if op.fn is None:
                            continue
                        ins = op.fn(eng)
                        if op.stream is not None:
                            ins.then_inc(ssem[(op.stream, op.sidx)], 16)
                        elif op.flag:
                            ins.then_inc(sems[e], 1)
                    if e == "sp":
                        for s in final_waits:
                            n = self.streams[s]
                            for i in range(min(NSEM, n)):
                                eng.wait_ge(ssem[(s, i)], 16 * ((n - 1 - i) // NSEM + 1))

                getattr(block, engmap[e])(body)


class K:
    def __init__(self, nc):
        self.nc = nc
        self.S = Sched(nc)
        self.ps = [nc.alloc_psum_tensor("ps%d" % i, [128, 512], F32) for i in range(8)]
        self.psn = 0
        self.uid = 0
        S = self.S
        self.ident = nc.alloc_sbuf_tensor("ident", [128, 128], F32)
        self.ones_f = nc.alloc_sbuf_tensor("ones_f", [128, 128], F32)
        self.ones_b = nc.alloc_sbuf_tensor("ones_b", [128, 128], BF16)
        self.mhalf = nc.alloc_sbuf_tensor("mhalf", [128, 1], F32)
        self.junk = nc.alloc_sbuf_tensor("junk", [128, 8, 128], BF16)
        self.modT = [nc.alloc_sbuf_tensor("modT%d" % l, [128, 48, 3], F32) for l in range(2)]
        ident = self.ident

        S.pool(lambda e: e.memset(ident[:], 0.0), w=["ident"])
        S.pool(lambda e: e.affine_select(out=ident[:], in_=ident[:], pattern=[[-1, 128]],
                                         compare_op=ALU.not_equal, fill=1.0, base=0, channel_multiplier=1),
               r=["ident"], w=["ident"])
        S.pool(lambda e: e.memset(self.ones_f[:], 1.0), w=["ones_f"])
        S.pool(lambda e: e.memset(self.ones_b[:], 1.0), w=["ones_b"])
        S.pool(lambda e: e.memset(self.mhalf[:], -0.5), w=["mhalf"])

    def bank(self):
        i = self.psn % 8
        self.psn += 1
        return i

    def sb(self, es, shape, dtype, name="t"):
        self.uid += 1
        return es.enter_context(self.nc.sbuf_tensor("%s_%d" % (name, self.uid), list(shape), dtype))


def mod_transpose(k, xin, xtok, ntile, hT, htok, l, sc_chunk0, sh_chunk0, row):
    S = k.S
    mT = k.modT[l]
    for kc in range(8):
        b = k.bank()
        for tt in range(ntile):
            S.pe(lambda e, b=b, tt=tt, kc=kc: e.transpose(
                k.ps[b][:, tt * 128:(tt + 1) * 128], xin[:, tt, kc * 128:(kc + 1) * 128], k.ident[:]),
                r=[(xtok, tt), "ident"], w=[("ps", b)])
        S.act(lambda e, b=b, kc=kc: e.activation(
            out=hT[:, kc, 0:ntile * 128], in_=k.ps[b][:, 0:ntile * 128], func=AF.Identity,
            bias=mT[:, sh_chunk0 + kc, row:row + 1], scale=mT[:, sc_chunk0 + kc, row:row + 1]),
            r=[("ps", b), ("modT", l)], w=[(htok, kc)])


def ln_epilogue(k, ybanks, xres, xtok, gate, gtok, lng, lnb, tmp, ttok, st, sttok, out_ap, otok, via_act=False,
                gb="pool", defer=False):
    S = k.S
    for h in range(2):
        if via_act:
            S.act(lambda e, h=h: e.activation(out=tmp[:, h * 512:(h + 1) * 512], in_=k.ps[ybanks[h]][:, :], func=AF.Copy),
                  r=[("ps", ybanks[h])], w=[(ttok, h)])
            S.pool(lambda e, h=h: e.tensor_tensor(out=tmp[:, h * 512:(h + 1) * 512], in0=tmp[:, h * 512:(h + 1) * 512],
                                                  in1=gate[:, h * 512:(h + 1) * 512], op=ALU.mult),
                   r=[(ttok, h), gtok], w=[(ttok, h)])
        else:
            S.dve(lambda e, h=h: e.tensor_tensor(out=tmp[:, h * 512:(h + 1) * 512], in0=k.ps[ybanks[h]][:, :],
                                                 in1=gate[:, h * 512:(h + 1) * 512], op=ALU.mult),
                  r=[("ps", ybanks[h]), gtok], w=[(ttok, h)])
    S.dve(lambda e: e.scalar_tensor_tensor(out=out_ap, in0=xres, scalar=ALPHA, in1=tmp[:, :],
                                           op0=ALU.mult, op1=ALU.add),
          r=[xtok, (ttok, 0), (ttok, 1)], w=[otok])
    for h in range(2):
        S.dve(lambda e, h=h: e.bn_stats(out=st[:, h * 6:(h + 1) * 6], in_=out_ap[:, h * 512:(h + 1) * 512]),
              r=[otok], w=[(sttok, h)])
    S.dve(lambda e: e.bn_aggr(out=st[:, 12:14], in_=st[:, 0:12]), r=[(sttok, 0), (sttok, 1)], w=[(sttok, 2)])
    S.dve(lambda e: e.tensor_scalar(out=st[:, 14:15], in0=st[:, 13:14], scalar1=LN_EPS, scalar2=None, op0=ALU.add),
          r=[(sttok, 2)], w=[(sttok, 3)])
    S.pool(lambda e: e.tensor_tensor(out=st[:, 15:16], in0=st[:, 14:15], in1=k.mhalf[:, 0:1], op=ALU.pow),
           r=[(sttok, 3), "mhalf"], w=[(sttok, 4)])
    S.dve(lambda e: e.scalar_tensor_tensor(out=st[:, 16:17], in0=st[:, 12:13], scalar=-1.0, in1=st[:, 15:16],
                                           op0=ALU.mult, op1=ALU.mult),
          r=[(sttok, 2), (sttok, 4)], w=[(sttok, 5)])
    S.act(lambda e: e.activation(out=out_ap, in_=out_ap, func=AF.Identity, bias=st[:, 16:17], scale=st[:, 15:16]),
          r=[otok, (sttok, 4), (sttok, 5)], w=[otok])
    def fin():
        S.add(gb, lambda e: e.tensor_tensor(out=out_ap, in0=out_ap, in1=lng[:, :], op=ALU.mult), r=[otok, "lng"], w=[otok])
        S.add(gb, lambda e: e.tensor_tensor(out=out_ap, in0=out_ap, in1=lnb[:, :], op=ALU.add), r=[otok, "lnb"], w=[otok])
    if defer:
        return fin
    fin()


def load_rows_bc(k, tile, dram_row, tok, eng="sp"):
    k.S.dma(eng, tile[:, :], dram_row.partition_broadcast(128), w=[tok])


def phase_mod(k, c_d, cctx_d, wmod_d, bmod_d, grow_d):
    S = k.S
    nc = k.nc
    with ExitStack() as es:
        c3 = k.sb(es, [3, D], F32, "c3")
        s3 = k.sb(es, [3, D], F32, "s3")
        scT = k.sb(es, [128, 8, 3], F32, "scT")
        b48 = k.sb(es, [48, 128], F32, "b48")
        bT = k.sb(es, [128, 48], F32, "bT")
        bias3 = k.sb(es, [3, 2, D], F32, "bias3")
        grow = k.sb(es, [3, D], F32, "grow")
        wsl = [k.sb(es, [128, 8, D], F32, "wsl") for _ in range(2)]
        S.dma("sp", c3[0:2, :], c_d, w=["c3"])
        S.dma("sp", c3[2:3, :], cctx_d.rearrange("(o n) -> o n", o=1), w=["c3"])
        S.act(lambda e: e.activation(out=s3[:, :], in_=c3[:, :], func=AF.Silu), r=["c3"], w=["s3"])
        b = k.bank()
        for kc in range(8):
            S.pe(lambda e, kc=kc, b=b: e.transpose(k.ps[b][:, kc * 3:(kc + 1) * 3], s3[0:3, kc * 128:(kc + 1) * 128],
                                                   k.ident[0:3, 0:3]), r=["s3", "ident"], w=[("ps", b)])
        S.dve(lambda e, b=b: e.tensor_copy(out=scT[:, :, :], in_=k.ps[b][:, 0:24]), r=[("ps", b)], w=["scT"])
        nsl = 0
        for l in range(2):
            S.dma("sp", b48[:, :], bmod_d[l].rearrange("(j p) -> j p", p=128), r=[], w=["b48"])
            b = k.bank()
            S.pe(lambda e, b=b: e.transpose(k.ps[b][:, 0:48], b48[0:48, :], k.ident[0:48, 0:48]),
                 r=["b48", "ident"], w=[("ps", b)])
            S.dve(lambda e, b=b: e.tensor_copy(out=bT[:, :], in_=k.ps[b][:, 0:48]), r=[("ps", b)], w=["bT"])
            for gi, s in enumerate((2, 5)):
                S.dma("sp", bias3[:, gi, :], bmod_d[l, s * D:(s + 1) * D].partition_broadcast(3), w=[("bias3", gi)])
            for s in range(6):
                wt = wsl[nsl % 2]
                wtok = ("wsl", nsl % 2)
                nsl += 1
                S.dma("sp", wt[:, :, :], wmod_d[l, :, s * D:(s + 1) * D].rearrange("(k p) n -> p k n", p=128), w=[wtok])
                b = k.bank()
                for j in range(8):
                    for kc in range(8):
                        S.pe(lambda e, b=b, j=j, kc=kc, wt=wt: e.matmul(
                            k.ps[b][:, j * 3:(j + 1) * 3], wt[:, kc, j * 128:(j + 1) * 128], scT[:, kc, :],
                            start=(kc == 0), stop=(kc == 7)), r=[wtok, "scT"], w=[("ps", b)])
                for r_ in range(3):
                    S.dve(lambda e, b=b, r_=r_, s=s, l=l: e.tensor_tensor(
                        out=k.modT[l][:, s * 8:(s + 1) * 8, r_], in0=k.ps[b][:, r_:24:3], in1=bT[:, s * 8:(s + 1) * 8],
                        op=ALU.add), r=[("ps", b), "bT"], w=[("modT", l)])
                if s in (2, 5):
                    gi = 0 if s == 2 else 1
                    for h in range(2):
                        b2 = k.bank()
                        for kc in range(8):
                            S.pe(lambda e, b2=b2, kc=kc, h=h, wt=wt: e.matmul(
                                k.ps[b2][0:3, :], scT[:, kc, :], wt[:, kc, h * 512:(h + 1) * 512],
                                start=(kc == 0), stop=(kc == 7)), r=[wtok, "scT"], w=[("ps", b2)])
                        S.dve(lambda e, b2=b2, h=h, gi=gi: e.tensor_tensor(
                            out=grow[:, h * 512:(h + 1) * 512], in0=k.ps[b2][0:3, :],
                            in1=bias3[:, gi, h * 512:(h + 1) * 512], op=ALU.add),
                            r=[("ps", b2), ("bias3", gi)], w=[("grow", h)])
                    S.dma("pool", grow_d[l, gi], grow[:, :], r=[("grow", 0), ("grow", 1)], stream="st")
            for c0 in (8, 32):
                S.dve(lambda e, l=l, c0=c0: e.tensor_scalar(
                    out=k.modT[l][:, c0:c0 + 8, :], in0=k.modT[l][:, c0:c0 + 8, :], scalar1=1.0, scalar2=None,
                    op0=ALU.add), r=[("modT", l)], w=[("modT", l)])
        S.barrier()


def phase_ffn(k, l, blocks, w1_d, w2_d, lng_d, lnb_d, grow_d):
    S = k.S
    with ExitStack() as es:
        w1 = k.sb(es, [128, 8, 2 * DFF], BF16, "w1")
        w2 = k.sb(es, [128, 22, D], BF16, "w2")
        lng = k.sb(es, [128, D], F32, "lng")
        lnb = k.sb(es, [128, D], F32, "lnb")
        gate = [k.sb(es, [128, D], F32, "gate") for _ in range(2)]
        xin = [k.sb(es, [128, 2, D], F32, "xin") for _ in range(2)]
        hT = [k.sb(es, [128, 8, 256], BF16, "hT") for _ in range(2)]
        actT = k.sb(es, [128, 22, 256], BF16, "actT")
        sg = [k.sb(es, [128, 256], BF16, "sg") for _ in range(2)]
        tmp = [k.sb(es, [128, D], F32, "tmp") for _ in range(2)]
        st = [k.sb(es, [128, 20], F32, "st") for _ in range(2)]
        for kc in range(8):
            S.dma("pool", w1[:, kc, :], w1_d[kc * 128:(kc + 1) * 128, :], w=[("w1", kc)], stream="ldw")
        for j in range(22):
            S.dma("pool", w2[:, j, :], w2_d[j * 128:(j + 1) * 128, :], w=[("w2", j)], stream="ldw")
        load_rows_bc(k, lng, lng_d, "lng")
        load_rows_bc(k, lnb, lnb_d, "lnb")
        cur_row = None
        ng = 0
        nt = 0
        def front(bi):
            in_ap, out_ap, row = blocks[bi]
            ntile = in_ap.shape[0] // 128
            x, xt = xin[bi % 2], ("xin", bi % 2)
            for tt in range(ntile):
                S.dma("sp", x[:, tt, :], in_ap[tt * 128:(tt + 1) * 128, :], w=[(xt, tt)])
            mod_transpose(k, x, xt, ntile, hT[bi % 2], ("hT", bi % 2), l, 32, 24, row)

        front(0)
        for bi, (in_ap, out_ap, row) in enumerate(blocks):
            T = in_ap.shape[0]
            ntile = T // 128
            if row != cur_row:
                g = gate[ng % 2]
                gtok = ("gate", ng % 2)
                ng += 1
                load_rows_bc(k, g, grow_d[l, 1, row], gtok)
                cur_row = row
            x = xin[bi % 2]
            xt = ("xin", bi % 2)
            h = hT[bi % 2]
            ht = ("hT", bi % 2)
            for j in range(22):
                bg = k.bank()
                bu = k.bank()
                for (bb, col) in ((bg, j * 128), (bu, DFF + j * 128)):
                    for kc in range(8):
                        S.pe(lambda e, bb=bb, col=col, kc=kc, h=h, T=T: e.matmul(
                            k.ps[bb][:, 0:T], w1[:, kc, col:col + 128], h[:, kc, 0:T],
                            start=(kc == 0), stop=(kc == 7)), r=[("w1", kc), (ht, kc)], w=[("ps", bb)])
                s_ = sg[j % 2]
                S.act(lambda e, bg=bg, s_=s_, T=T: e.activation(out=s_[:, 0:T], in_=k.ps[bg][:, 0:T], func=AF.Silu),
                      r=[("ps", bg)], w=[("sg", j % 2)])
                S.dve(lambda e, bu=bu, s_=s_, j=j, T=T: e.tensor_tensor(
                    out=actT[:, j, 0:T], in0=k.ps[bu][:, 0:T], in1=s_[:, 0:T], op=ALU.mult),
                    r=[("ps", bu), ("sg", j % 2)], w=[("actT", j)])
            if bi + 1 < len(blocks):
                front(bi + 1)
            for tt in range(ntile):
                yb = (k.bank(), k.bank())
                for hh in range(2):
                    for j in range(22):
                        S.pe(lambda e, hh=hh, j=j, tt=tt, yb=yb: e.matmul(
                            k.ps[yb[hh]][:, :], actT[:, j, tt * 128:(tt + 1) * 128], w2[:, j, hh * 512:(hh + 1) * 512],
                            start=(j == 0), stop=(j == 21)), r=[("actT", j), ("w2", j)], w=[("ps", yb[hh])])
                ln_epilogue(k, yb, x[:, tt, :], (xt, tt), g, gtok, lng, lnb, tmp[nt % 2], ("tmp", nt % 2),
                            st[nt % 2], ("st", nt % 2), x[:, tt, :], (xt, tt))
                nt += 1
                S.dma("pool", out_ap[tt * 128:(tt + 1) * 128, :], x[:, tt, :], r=[(xt, tt)], stream="st")
        S.barrier()


APAD = 15
A_CTX0 = APAD
A_LAT0 = APAD + CTX + 2 * APAD
A_LEN = A_LAT0 + SEQ + APAD


def seq_blocks(T):
    out = [(0, CTX, True)] if T >= CTX else [(t, T, True) for t in range(0, CTX, T)]
    out += [(CTX + t, T, False) for t in range(0, SEQ, T)]
    return out


def tok_src(x_d, ctx_d, b, t0, n):
    if t0 < CTX:
        return ctx_d[b, t0:t0 + n, :]
    return x_d[b, t0 - CTX:t0 - CTX + n, :]


def phase_l0a(k, x_d, ctx_d, wab_d, AT_d, UT_d):
    S = k.S
    with ExitStack() as es:
        wab = k.sb(es, [128, 8, 1536], BF16, "wab")
        xin = [k.sb(es, [128, 4, D], F32, "xin") for _ in range(2)]
        hT = [k.sb(es, [128, 8, 512], BF16, "hT") for _ in range(2)]
        aTs = [k.sb(es, [128, 4, 512], BF16, "aTs") for _ in range(2)]
        uTs = [k.sb(es, [128, 4, 512], BF16, "uTs") for _ in range(2)]
        sg = [k.sb(es, [128, 512], BF16, "sg") for _ in range(2)]
        zt = k.sb(es, [128, 4, 2 * APAD], BF16, "zt")
        for kc in range(8):
            S.dma("pool", wab[:, kc, :], wab_d[kc * 128:(kc + 1) * 128, :], w=[("wab", kc)], stream="ldw")
        S.dve(lambda e: e.memset(zt[:, :, :], 0.0), w=["zt"])
        bi = 0
        for b in range(NSEQ):
            for (o, n) in ((0, APAD), (A_CTX0 + CTX, 2 * APAD), (A_LAT0 + SEQ, APAD)):
                S.dma("pool", AT_d[b][:, :, o:o + n], zt[:, :, 0:n], r=["zt"], stream="st")
            for (t0, T, isctx) in seq_blocks(512):
                row = 2 if isctx else b
                ntile = T // 128
                x = xin[bi % 2]
                xt = ("xin", bi % 2)
                h = hT[bi % 2]
                ht = ("hT", bi % 2)
                a_ = aTs[bi % 2]
                u_ = uTs[bi % 2]
                for tt in range(ntile):
                    S.dma("sp", x[:, tt, :], tok_src(x_d, ctx_d, b, t0 + tt * 128, 128), w=[(xt, tt)])
                mod_transpose(k, x, xt, ntile, h, ht, 0, 8, 0, row)
                for mc in (4, 0, 5, 1, 6, 2, 7, 3, 8, 9, 10, 11):
                    bk = k.bank()
                    for kc in range(8):
                        S.pe(lambda e, bk=bk, mc=mc, kc=kc, h=h, T=T: e.matmul(
                            k.ps[bk][:, 0:T], wab[:, kc, mc * 128:(mc + 1) * 128], h[:, kc, 0:T],
                            start=(kc == 0), stop=(kc == 7)), r=[("wab", kc), (ht, kc)], w=[("ps", bk)])
                    if 4 <= mc < 8:
                        s_ = sg[mc % 2]
                        S.act(lambda e, bk=bk, s_=s_, T=T: e.activation(out=s_[:, 0:T], in_=k.ps[bk][:, 0:T], func=AF.Sigmoid),
                              r=[("ps", bk)], w=[("sg", mc % 2)])
                    elif mc < 4:
                        s_ = sg[mc % 2]
                        S.dve(lambda e, bk=bk, s_=s_, T=T, mc=mc, a_=a_: e.tensor_tensor(
                            out=a_[:, mc, 0:T], in0=k.ps[bk][:, 0:T], in1=s_[:, 0:T], op=ALU.mult),
                            r=[("ps", bk), ("sg", mc % 2)], w=[("aTs", bi % 2, mc)])
                    else:
                        S.act(lambda e, bk=bk, T=T, mc=mc, u_=u_: e.activation(
                            out=u_[:, mc - 8, 0:T], in_=k.ps[bk][:, 0:T], func=AF.Copy),
                            r=[("ps", bk)], w=[("uTs", bi % 2, mc - 8)])
                ao = (A_CTX0 + t0) if isctx else (A_LAT0 + t0 - CTX)
                S.dma("pool", AT_d[b][:, :, ao:ao + T], a_[:, :, 0:T], r=[("aTs", bi % 2, q) for q in range(4)], stream="st")
                S.dma("pool", UT_d[b][:, :, t0:t0 + T], u_[:, :, 0:T], r=[("uTs", bi % 2, q) for q in range(4)], stream="st")
                bi += 1
        S.barrier()


def load_cols(k, es, rows_aps, name):
    S = k.S
    nr = len(rows_aps)
    q = rows_aps[0].shape[0] // 128
    rt = k.sb(es, [nr * q, 128], F32, name + "r")
    ct = k.sb(es, [128, nr, q], F32, name)
    for i, ap in enumerate(rows_aps):
        S.dma("sp", rt[i * q:(i + 1) * q, :], ap.rearrange("(j p) -> j p", p=128), w=[(name, "r", i)])
    b = k.bank()
    S.pe(lambda e: e.transpose(k.ps[b][:, 0:nr * q], rt[0:nr * q, :], k.ident[0:nr * q, 0:nr * q]),
         r=[(name, "r", i) for i in range(nr)] + ["ident"], w=[("ps", b)])
    S.dve(lambda e: e.tensor_copy(out=ct[:, :, :], in_=k.ps[b][:, 0:nr * q]), r=[("ps", b)], w=[name])
    return ct


def phase_conv(k, AT_d, CT_d, convw_d, convb_d, cg_d, cb_d):
    S = k.S
    with ExitStack() as es:
        cols = load_cols(k, es, [convb_d, cg_d, cb_d], "ccols")
        cw31 = k.sb(es, [31, 512], F32, "cw31")
        cwT = k.sb(es, [128, 4, 31], F32, "cwT")
        dg = k.sb(es, [128, 4, 31, 128], BF16, "dg")
        ain = [k.sb(es, [128, 4, 512 + 2 * APAD], BF16, "ain") for _ in range(2)]
        xc = k.sb(es, [128, 4, 512], F32, "xc")
        xsq = k.sb(es, [128, 4, 512], F32, "xsq")
        mean = k.sb(es, [128, 512], F32, "mean")
        rstd = k.sb(es, [128, 512], F32, "rstd")
        cs = [k.sb(es, [128, 4, 512], BF16, "cs") for _ in range(2)]
        S.dma("sp", cw31[:, :], convw_d, w=["cw31"])
        for q in range(4):
            b = k.bank()
            S.pe(lambda e, q=q, b=b: e.transpose(k.ps[b][:, 0:31], cw31[0:31, q * 128:(q + 1) * 128], k.ident[0:31, 0:31]),
                 r=["cw31", "ident"], w=[("ps", b)])
            S.dve(lambda e, q=q, b=b: e.tensor_copy(out=cwT[:, q, :], in_=k.ps[b][:, 0:31]), r=[("ps", b)], w=["cwT"])
        for q in range(4):
            for t in range(31):
                eng = S.dve if (t % 2 == 0) else S.pool
                eng(lambda e, q=q, t=t: e.tensor_scalar(out=dg[:, q, t, :], in0=k.ident[:, :], scalar1=cwT[:, q, t:t + 1],
                                                        scalar2=None, op0=ALU.mult), r=["cwT", "ident"], w=[("dg", q)])
        bi = 0
        for b_ in range(NSEQ):
            for (t0, T, isctx) in seq_blocks(512):
                a = ain[bi % 2]
                at = ("ain", bi % 2)
                c_ = cs[bi % 2]
                ao = (A_CTX0 + t0) if isctx else (A_LAT0 + t0 - CTX)
                S.dma("sp", a[:, :, 0:T + 2 * APAD], AT_d[b_][:, :, ao - APAD:ao + T + APAD], w=[at])
                for q in range(4):
                    bk = k.bank()
                    for t in range(31):
                        S.pe(lambda e, bk=bk, q=q, t=t, a=a, T=T: e.matmul(
                            k.ps[bk][:, 0:T], dg[:, q, t, :], a[:, q, t:t + T], start=(t == 0), stop=(t == 30)),
                            r=[("dg", q), at], w=[("ps", bk)])
                    S.act(lambda e, bk=bk, q=q, T=T: e.activation(out=xc[:, q, 0:T], in_=k.ps[bk][:, 0:T], func=AF.Identity,
                                                                  bias=cols[:, 0, q:q + 1], scale=1.0),
                          r=[("ps", bk), "ccols"], w=[("xc", q)])
                    S.act(lambda e, q=q, T=T: e.activation(out=xsq[:, q, 0:T], in_=xc[:, q, 0:T], func=AF.Square),
                          r=[("xc", q)], w=[("xsq", q)])
                b1 = k.bank()
                b2 = k.bank()
                for q in range(4):
                    S.pe(lambda e, q=q, b1=b1, T=T: e.matmul(k.ps[b1][:, 0:T], k.ones_f[:, :], xc[:, q, 0:T],
                                                             start=(q == 0), stop=(q == 3)),
                         r=["ones_f", ("xc", q)], w=[("ps", b1)])
                for q in range(4):
                    S.pe(lambda e, q=q, b2=b2, T=T: e.matmul(k.ps[b2][:, 0:T], k.ones_f[:, :], xsq[:, q, 0:T],
                                                             start=(q == 0), stop=(q == 3)),
                         r=["ones_f", ("xsq", q)], w=[("ps", b2)])
                S.act(lambda e, b1=b1, T=T: e.activation(out=mean[:, 0:T], in_=k.ps[b1][:, 0:T], func=AF.Copy, scale=1.0 / 512),
                      r=[("ps", b1)], w=["mean"])
                S.dve(lambda e, T=T: e.tensor_tensor(out=rstd[:, 0:T], in0=mean[:, 0:T], in1=mean[:, 0:T], op=ALU.mult),
                      r=["mean"], w=["rstd"])
                S.dve(lambda e, b2=b2, T=T: e.scalar_tensor_tensor(out=rstd[:, 0:T], in0=k.ps[b2][:, 0:T], scalar=1.0 / 512,
                                                                   in1=rstd[:, 0:T], op0=ALU.mult, op1=ALU.subtract),
                      r=[("ps", b2), "rstd"], w=["rstd"])
                S.act(lambda e, T=T: e.activation(out=rstd[:, 0:T], in_=rstd[:, 0:T], func=AF.Ln, bias=LN_EPS, scale=1.0),
                      r=["rstd"], w=["rstd"])
                S.act(lambda e, T=T: e.activation(out=rstd[:, 0:T], in_=rstd[:, 0:T], func=AF.Exp, scale=-0.5),
                      r=["rstd"], w=["rstd"])
                for q in range(4):
                    S.dve(lambda e, q=q, T=T: e.tensor_tensor(out=xc[:, q, 0:T], in0=xc[:, q, 0:T], in1=mean[:, 0:T], op=ALU.subtract),
                          r=[("xc", q), "mean"], w=[("xc", q)])
                    (S.pool if q % 2 else S.dve)(lambda e, q=q, T=T: e.tensor_tensor(out=xc[:, q, 0:T], in0=xc[:, q, 0:T], in1=rstd[:, 0:T], op=ALU.mult),
                                                 r=[("xc", q), "rstd"], w=[("xc", q)])
                    S.act(lambda e, q=q, T=T, c_=c_: e.activation(out=c_[:, q, 0:T], in_=xc[:, q, 0:T], func=AF.Silu,
                                                                  bias=cols[:, 2, q:q + 1], scale=cols[:, 1, q:q + 1]),
                          r=[("xc", q), "ccols"], w=[("cs", bi % 2, q)])
                S.dma("pool", CT_d[b_][:, :, t0:t0 + T], c_[:, :, 0:T], r=[("cs", bi % 2, q) for q in range(4)], stream="st")
                bi += 1
        S.barrier()


def phase_outproj(k, l, srcs, w_d, lng_d, lnb_d, grow_d, blocks):
    S = k.S
    with ExitStack() as es:
        wo = k.sb(es, [128, 8, D], BF16, "wo")
        lng = k.sb(es, [128, D], F32, "lng")
        lnb = k.sb(es, [128, D], F32, "lnb")
        gate = [k.sb(es, [128, D], F32, "gate") for _ in range(2)]
        src = [k.sb(es, [128, 8, 512], BF16, "src") for _ in range(2)]
        xin = [k.sb(es, [128, 4, D], F32, "xin") for _ in range(2)]
        tmp = [k.sb(es, [128, D], F32, "tmp") for _ in range(2)]
        st = [k.sb(es, [128, 20], F32, "st") for _ in range(2)]
        for kc in range(8):
            S.dma("pool", wo[:, kc, :], w_d[kc * 128:(kc + 1) * 128, :], w=[("wo", kc)], stream="ldw")
        load_rows_bc(k, lng, lng_d, "lng")
        load_rows_bc(k, lnb, lnb_d, "lnb")
        pending = None
        cur_row = None
        ng = 0
        nt = 0
        for bi, (src_aps, res_ap, out_ap, row) in enumerate(blocks):
            T = res_ap.shape[0]
            ntile = T // 128
            if row != cur_row:
                g = gate[ng % 2]
                gtok = ("gate", ng % 2)
                ng += 1
                load_rows_bc(k, g, grow_d[l, 0, row], gtok)
                cur_row = row
            s_ = src[bi % 2]
            stok = ("src", bi % 2)
            x = xin[bi % 2]
            xt = ("xin", bi % 2)
            c0 = 0
            for ap in src_aps:
                nch = ap.shape[1]
                S.dma("sp", s_[:, c0:c0 + nch, 0:T], ap, w=[(stok, c0)])
                c0 += nch
            srd = [(stok, c) for c in (0, 4)] if len(src_aps) == 2 else [(stok, 0)]
            for tt in range(ntile):
                S.dma("sp", x[:, tt, :], res_ap[tt * 128:(tt + 1) * 128, :], w=[(xt, tt)])
            for tt in range(ntile):
                yb = (k.bank(), k.bank())
                for hh in range(2):
                    for kc in range(8):
                        S.pe(lambda e, hh=hh, kc=kc, tt=tt, yb=yb, s_=s_: e.matmul(
                            k.ps[yb[hh]][:, :], s_[:, kc, tt * 128:(tt + 1) * 128], wo[:, kc, hh * 512:(hh + 1) * 512],
                            start=(kc == 0), stop=(kc == 7)), r=srd + [("wo", kc)], w=[("ps", yb[hh])])
                fin = ln_epilogue(k, yb, x[:, tt, :], (xt, tt), g, gtok, lng, lnb, tmp[nt % 2], ("tmp", nt % 2),
                                  st[nt % 2], ("st", nt % 2), x[:, tt, :], (xt, tt), gb="dve", defer=True)
                nt += 1
                if pending is not None:
                    pending()

                def pending(fin=fin, out_ap=out_ap, x=x, xt=xt, tt=tt):
                    fin()
                    S.dma("pool", out_ap[tt * 128:(tt + 1) * 128, :], x[:, tt, :], r=[(xt, tt)], stream="st")
        if pending is not None:
            pending()
        S.barrier()


TAU = 8
PREP_STAGE = 99
NCH = NTOK // TAU
MAGIC = 12582912.0
TWO_PI = 2.0 * math.pi


def bc(ap, shape):
    return ap.to_broadcast(list(shape))


def sincos(k, es, ang, n, tok):
    S = k.S
    outs = []
    for name, shift in (("sin", 0.0), ("cos", 0.5 * math.pi)):
        kk = k.sb(es, [128, n], F32, "kk" + name)
        rr = k.sb(es, [128, n], F32, "rr" + name)
        res = k.sb(es, [128, n], F32, "res" + name)
        S.dve(lambda e, kk=kk, shift=shift: e.tensor_scalar(out=kk[:, :], in0=ang, scalar1=1.0 / TWO_PI,
                                                            scalar2=shift / TWO_PI, op0=ALU.mult, op1=ALU.add),
              r=[tok], w=[(tok, name, "kk")])
        S.dve(lambda e, kk=kk: e.tensor_scalar(out=kk[:, :], in0=kk[:, :], scalar1=MAGIC, scalar2=None, op0=ALU.add),
              r=[(tok, name, "kk")], w=[(tok, name, "kk")])
        S.dve(lambda e, kk=kk: e.tensor_scalar(out=kk[:, :], in0=kk[:, :], scalar1=-MAGIC, scalar2=None, op0=ALU.add),
              r=[(tok, name, "kk")], w=[(tok, name, "kk")])
        S.dve(lambda e, kk=kk, rr=rr: e.scalar_tensor_tensor(out=rr[:, :], in0=kk[:, :], scalar=-TWO_PI, in1=ang,
                                                             op0=ALU.mult, op1=ALU.add),
              r=[(tok, name, "kk"), tok], w=[(tok, name, "rr")])
        S.dve(lambda e, rr=rr, shift=shift: e.tensor_scalar(out=rr[:, :], in0=rr[:, :], scalar1=shift, scalar2=None, op0=ALU.add),
              r=[(tok, name, "rr")], w=[(tok, name, "rr")])
        S.dve(lambda e, rr=rr: e.tensor_scalar(out=rr[:, :], in0=rr[:, :], scalar1=-3.14159, scalar2=3.14159,
                                               op0=ALU.max, op1=ALU.min),
              r=[(tok, name, "rr")], w=[(tok, name, "rr")])
        S.act(lambda e, rr=rr, res=res: e.activation(out=res[:, :], in_=rr[:, :], func=AF.Sin),
              r=[(tok, name, "rr")], w=[(tok, name)])
        outs.append(res)
    return outs


def phase_s5prep(k, T, p):
    S = k.S
    with ExitStack() as es:
        rows = k.sb(es, [64, 2, 2, 64], F32, "lrows")
        lamT = k.sb(es, [128, 2, 64], F32, "lamT")
        dtb = k.sb(es, [128, 64], F32, "dtb")
        ell = k.sb(es, [128, 64], F32, "ell")
        phi = k.sb(es, [128, 64], F32, "phi")
        kvec = k.sb(es, [128, 9], F32, "kvec")
        ang = k.sb(es, [128, 64, 9], F32, "ang")
        mag = k.sb(es, [128, 64, 9], F32, "mag")
        lre = k.sb(es, [128, 64, 9], F32, "lre")
        lim = k.sb(es, [128, 64, 9], F32, "lim")
        for ri, nm in enumerate(("lam_re", "lam_im")):
            for dup in range(2):
                S.dma("sp", rows[:, ri, dup, :], p[nm].rearrange("d g p -> (d g) p"), w=[("rows", ri, dup)])
        for ri in range(2):
            b = k.bank()
            S.pe(lambda e, ri=ri, b=b: e.transpose(k.ps[b][:, 0:64], rows[0:64, ri, :, :].rearrange("p a b -> p (a b)"), k.ident[0:64, 0:64]),
                 r=[("rows", ri, 0), ("rows", ri, 1), "ident"], w=[("ps", b)])
            S.dve(lambda e, ri=ri, b=b: e.tensor_copy(out=lamT[:, ri, :], in_=k.ps[b][:, 0:64]), r=[("ps", b)], w=["lamT"])
        S.dma("sp", dtb[:, :], p["log_dt"].rearrange("d g -> (d g)").partition_broadcast(128), w=["dtb"])
        S.act(lambda e: e.activation(out=dtb[:, :], in_=dtb[:, :], func=AF.Exp), r=["dtb"], w=["dtb"])
        S.dve(lambda e: e.tensor_tensor(out=ell[:, :], in0=lamT[:, 0, :], in1=dtb[:, :], op=ALU.mult), r=["lamT", "dtb"], w=["ell"])
        S.dve(lambda e: e.tensor_tensor(out=phi[:, :], in0=lamT[:, 1, :], in1=dtb[:, :], op=ALU.mult), r=["lamT", "dtb"], w=["phi"])
        for i in range(9):
            S.pool(lambda e, i=i: e.memset(kvec[:, i:i + 1], float(i)), w=["kvec"])
        S.dve(lambda e: e.tensor_tensor(out=ang[:, :, :], in0=bc(phi[:, :].unsqueeze(2), [128, 64, 9]),
                                        in1=bc(kvec[:, :].unsqueeze(1), [128, 64, 9]), op=ALU.mult),
              r=["phi", "kvec"], w=["ang"])
        S.dve(lambda e: e.tensor_tensor(out=mag[:, :, :], in0=bc(ell[:, :].unsqueeze(2), [128, 64, 9]),
                                        in1=bc(kvec[:, :].unsqueeze(1), [128, 64, 9]), op=ALU.mult),
              r=["ell", "kvec"], w=["mag"])
        S.act(lambda e: e.activation(out=mag[:, :, :], in_=mag[:, :, :], func=AF.Exp), r=["mag"], w=["mag"])
        sn, cs = sincos(k, es, ang[:, :, :].rearrange("p a b -> p (a b)"), 576, "ang")
        S.dve(lambda e: e.tensor_tensor(out=lre[:, :, :].rearrange("p a b -> p (a b)"), in0=mag[:, :, :].rearrange("p a b -> p (a b)"),
                                        in1=cs[:, :], op=ALU.mult), r=["mag", ("ang", "cos")], w=["lre"])
        S.dve(lambda e: e.tensor_tensor(out=lim[:, :, :].rearrange("p a b -> p (a b)"), in0=mag[:, :, :].rearrange("p a b -> p (a b)"),
                                        in1=sn[:, :], op=ALU.mult), r=["mag", ("ang", "sin")], w=["lim"])
        if PREP_STAGE <= 1:
            S.barrier()
            return
        for d in range(2):
            S.dve(lambda e, d=d: e.tensor_copy(out=T["LRm"][:, :, d, :, :],
                                               in_=bc(lre[0:64, d * 32:(d + 1) * 32, 8:9].unsqueeze(1), [64, 2, 32, 2])),
                  r=["lre"], w=["LRm"])
            S.dve(lambda e, d=d: e.tensor_scalar(out=T["LIm"][:, 0, d, :, :], in0=bc(lim[0:64, d * 32:(d + 1) * 32, 8:9], [64, 32, 2]),
                                                 scalar1=-1.0, scalar2=None, op0=ALU.mult), r=["lim"], w=["LIm"])
            S.dve(lambda e, d=d: e.tensor_copy(out=T["LIm"][:, 1, d, :, :], in_=bc(lim[0:64, d * 32:(d + 1) * 32, 8:9], [64, 32, 2])),
                  r=["lim"], w=["LIm"])
        nre = k.sb(es, [128, 64], F32, "nre")
        den = k.sb(es, [128, 64], F32, "den")
        t1 = k.sb(es, [128, 64], F32, "t1")
        kre = k.sb(es, [128, 64], F32, "kre")
        kim = k.sb(es, [128, 64], F32, "kim")
        L0, L1 = lamT[:, 0, :], lamT[:, 1, :]
        S.dve(lambda e: e.tensor_scalar(out=nre[:, :], in0=lre[:, :, 1], scalar1=-1.0, scalar2=None, op0=ALU.add), r=["lre"], w=["nre"])
        S.dve(lambda e: e.tensor_tensor(out=den[:, :], in0=L0, in1=L0, op=ALU.mult), r=["lamT"], w=["den"])
        S.dve(lambda e: e.tensor_tensor(out=t1[:, :], in0=L1, in1=L1, op=ALU.mult), r=["lamT"], w=["t1"])
        S.dve(lambda e: e.tensor_tensor(out=den[:, :], in0=den[:, :], in1=t1[:, :], op=ALU.add), r=["den", "t1"], w=["den"])
        S.dve(lambda e: e.reciprocal(out=den[:, :], in_=den[:, :]), r=["den"], w=["den"])
        S.dve(lambda e: e.tensor_tensor(out=kre[:, :], in0=nre[:, :], in1=L0, op=ALU.mult), r=["nre", "lamT"], w=["kre"])
        S.dve(lambda e: e.tensor_tensor(out=t1[:, :], in0=lim[:, :, 1], in1=L1, op=ALU.mult), r=["lim", "lamT", "den"], w=["t1"])
        S.dve(lambda e: e.tensor_tensor(out=kre[:, :], in0=kre[:, :], in1=t1[:, :], op=ALU.add), r=["kre", "t1"], w=["kre"])
        S.dve(lambda e: e.tensor_tensor(out=kre[:, :], in0=kre[:, :], in1=den[:, :], op=ALU.mult), r=["kre", "den"], w=["kre"])
        S.dve(lambda e: e.tensor_tensor(out=kim[:, :], in0=lim[:, :, 1], in1=L0, op=ALU.mult), r=["lim", "lamT"], w=["kim"])
        S.dve(lambda e: e.tensor_tensor(out=t1[:, :], in0=nre[:, :], in1=L1, op=ALU.mult), r=["nre", "lamT", "kre"], w=["t1"])
        S.dve(lambda e: e.tensor_tensor(out=kim[:, :], in0=kim[:, :], in1=t1[:, :], op=ALU.subtract), r=["kim", "t1"], w=["kim"])
        S.dve(lambda e: e.tensor_tensor(out=kim[:, :], in0=kim[:, :], in1=den[:, :], op=ALU.mult), r=["kim", "den"], w=["kim"])
        Y = k.sb(es, [128, 2, 64, 16], F32, "Y")
        X = k.sb(es, [128, 2, 64, 16], F32, "X")
        es1 = ExitStack()
        bp = k.sb(es1, [128, 2, 64, 16], F32, "bp")
        bb = k.sb(es1, [128, 2, 64, 16], F32, "bb")
        tb = k.sb(es1, [128, 64, 16], F32, "tb")
        for ri, nm in enumerate(("b_re", "b_im")):
            for half in range(2):
                for d in range(2):
                    S.dma("sp", bp[half * 64:(half + 1) * 64, ri, d * 32:(d + 1) * 32, :],
                          p[nm][d].rearrange("g p h -> p g h"), w=[("bp", ri)])
        kre3 = bc(kre[:, :].unsqueeze(2), [128, 64, 16])
        kim3 = bc(kim[:, :].unsqueeze(2), [128, 64, 16])
        S.dve(lambda e: e.tensor_tensor(out=bb[:, 0, :, :], in0=bp[:, 0, :, :], in1=kre3, op=ALU.mult), r=[("bp", 0), "kre"], w=[("bb", 0)])
        S.dve(lambda e: e.tensor_tensor(out=tb[:, :, :], in0=bp[:, 1, :, :], in1=kim3, op=ALU.mult), r=[("bp", 1), "kim"], w=["tb"])
        S.dve(lambda e: e.tensor_tensor(out=bb[:, 0, :, :], in0=bb[:, 0, :, :], in1=tb[:, :, :], op=ALU.subtract), r=[("bb", 0), "tb"], w=[("bb", 0)])
        S.dve(lambda e: e.tensor_tensor(out=bb[:, 1, :, :], in0=bp[:, 1, :, :], in1=kre3, op=ALU.mult), r=[("bp", 1), "kre"], w=[("bb", 1)])
        S.dve(lambda e: e.tensor_tensor(out=tb[:, :, :], in0=bp[:, 0, :, :], in1=kim3, op=ALU.mult), r=[("bp", 0), "kim", ("bb", 0)], w=["tb"])
        S.dve(lambda e: e.tensor_tensor(out=bb[:, 1, :, :], in0=bb[:, 1, :, :], in1=tb[:, :, :], op=ALU.add), r=[("bb", 1), "tb"], w=[("bb", 1)])
        S.dve(lambda e: e.tensor_copy(out=Y[0:64, 0, :, :], in_=bb[0:64, 0, :, :]), r=[("bb", 0)], w=[("Y", 0)])
        S.dve(lambda e: e.tensor_scalar(out=Y[0:64, 1, :, :], in0=bb[0:64, 1, :, :], scalar1=-1.0, scalar2=None, op0=ALU.mult), r=[("bb", 1)], w=[("Y", 1)])
        S.dve(lambda e: e.tensor_copy(out=Y[64:128, 0, :, :], in_=bb[64:128, 1, :, :]), r=[("bb", 1)], w=[("Y", 2)])
        S.dve(lambda e: e.tensor_copy(out=Y[64:128, 1, :, :], in_=bb[64:128, 0, :, :]), r=[("bb", 0)], w=[("Y", 3)])
        Ytok = [("Y", i) for i in range(4)]
        if PREP_STAGE <= 2:
            S.barrier()
            return
        crow = k.sb(es1, [128, 2, 8, 2, 64], F32, "crow")
        cT = k.sb(es1, [128, 2, 64, 16], F32, "cT")
        for ri, nm in enumerate(("c_re", "c_im")):
            for dup in range(2):
                S.dma("sp", crow[:, ri, :, dup, :], p[nm].rearrange("d g h p -> (d g h) p").rearrange("(t r) p -> r t p", r=128),
                      w=[("crow", ri, dup)])
            for t in range(8):
                b = k.bank()
                S.pe(lambda e, ri=ri, t=t, b=b: e.transpose(k.ps[b][:, 0:128], crow[:, ri, t, :, :].rearrange("p a b -> p (a b)"), k.ident[:, :]),
                     r=[("crow", ri, 0), ("crow", ri, 1), "ident"], w=[("ps", b)])
                S.act(lambda e, ri=ri, t=t, b=b: e.activation(out=cT[:, ri, t * 8:(t + 1) * 8, :], in_=k.ps[b][:, 0:128], func=AF.Copy),
                      r=[("ps", b)], w=[("cT", ri)])
        S.dve(lambda e: e.tensor_copy(out=X[0:64, 0, :, :], in_=cT[0:64, 0, :, :]), r=[("cT", 0)], w=[("X", 0)])
        S.dve(lambda e: e.tensor_scalar(out=X[0:64, 1, :, :], in0=cT[0:64, 1, :, :], scalar1=-1.0, scalar2=None, op0=ALU.mult), r=[("cT", 1)], w=[("X", 1)])
        S.dve(lambda e: e.tensor_scalar(out=X[64:128, 0, :, :], in0=cT[64:128, 1, :, :], scalar1=-1.0, scalar2=None, op0=ALU.mult), r=[("cT", 1)], w=[("X", 2)])
        S.dve(lambda e: e.tensor_scalar(out=X[64:128, 1, :, :], in0=cT[64:128, 0, :, :], scalar1=-1.0, scalar2=None, op0=ALU.mult), r=[("cT", 0)], w=[("X", 3)])
        Xtok = [("X", i) for i in range(4)]
        S.barrier()
        es1.close()
        if PREP_STAGE <= 3:
            S.barrier()
            return
        S.pool(lambda e: e.memset(T["Wfar"][:, :, :, :], 0.0), w=["Wfar"])
        CF = k.sb(es, [128, 9, 32, 16], F32, "CF")
        BL = k.sb(es, [128, 9, 32, 16], F32, "BL")
        tq = k.sb(es, [128, 9, 32, 16], F32, "tq")
        for d in range(2):
            gs = slice(d * 32, (d + 1) * 32)
            lre4 = bc(lre[:, gs, :].rearrange("p g k -> p k g").unsqueeze(3), [128, 9, 32, 16])
            lim4 = bc(lim[:, gs, :].rearrange("p g k -> p k g").unsqueeze(3), [128, 9, 32, 16])
            for (dst, src, stok, dtok) in ((CF, X, Xtok, "CF"), (BL, Y, Ytok, "BL")):
                s1 = bc(src[:, 0, gs, :].unsqueeze(1), [128, 9, 32, 16])
                s2 = bc(src[:, 1, gs, :].unsqueeze(1), [128, 9, 32, 16])
                S.dve(lambda e, dst=dst, s1=s1, lre4=lre4: e.tensor_tensor(out=dst[:, :, :, :], in0=s1, in1=lre4, op=ALU.mult),
                      r=stok + ["lre"], w=[dtok])
                S.pool(lambda e, s2=s2, lim4=lim4: e.tensor_tensor(out=tq[:, :, :, :], in0=s2, in1=lim4, op=ALU.mult),
                       r=stok + ["lim"], w=["tq"])
                S.dve(lambda e, dst=dst: e.tensor_tensor(out=dst[:, :, :, :], in0=dst[:, :, :, :], in1=tq[:, :, :, :], op=ALU.add),
                      r=[dtok, "tq"], w=[dtok])
            for par in range(2 if PREP_STAGE > 4 else 0):
                for j in range(TAU):
                    kk_ = j + 1 if d == 0 else TAU - j
                    S.act(lambda e, par=par, j=j, kk_=kk_, d=d: e.activation(
                        out=T["Wfar"][:, d * 32 + par:(d + 1) * 32:2, j, 16 * par:16 * par + 16],
                        in_=CF[:, kk_, par:32:2, :], func=AF.Copy), r=["CF", "Wfar"], w=["Wfar"])
            for q in range(4 if PREP_STAGE > 5 else 0):
                for lag in range(TAU):
                    b = k.bank()
                    S.pe(lambda e, b=b, q=q, lag=lag: e.matmul(k.ps[b][:, 0:128], BL[:, lag, q * 8:(q + 1) * 8, :].rearrange("p g h -> p (g h)"),
                                                               CF[:, 0, q * 8:(q + 1) * 8, :].rearrange("p g h -> p (g h)"), start=True, stop=True),
                         r=["BL", "CF"], w=[("ps", b)])
                    S.dve(lambda e, b=b, q=q, lag=lag, d=d: e.tensor_tensor(out=T["Knear"][:, q, d * 8 + lag, :], in0=k.ps[b][:, 0:128],
                                                                            in1=T["bdmask"][:, :], op=ALU.mult),
                          r=[("ps", b), "bdmask"], w=["Knear"])
                    b2 = k.bank()
                    S.pe(lambda e, b2=b2, q=q, lag=lag: e.transpose(k.ps[b2][:, 0:128], BL[:, lag, q * 8:(q + 1) * 8, :].rearrange("p g h -> p (g h)"), k.ident[:, :]),
                         r=["BL", "ident"], w=[("ps", b2)])
                    i_ = (TAU - 1 - lag) if d == 0 else lag
                    for par in range(2 if PREP_STAGE > 6 else 0):
                        S.act(lambda e, b2=b2, q=q, i_=i_, par=par, d=d: e.activation(
                            out=T["Wup"][:, par, d * 4 + q, i_, :], in_=k.ps[b2][:, 0:128], func=AF.Identity,
                            bias=0.0, scale=T["pmask"][:, par:par + 1]), r=[("ps", b2), "pmask"], w=["Wup"])
        S.barrier()


def rev_axis(ap, axis):
    pat = [list(x) for x in ap.ap]
    st, n = pat[axis]
    off = ap.offset + st * (n - 1)
    pat[axis] = [-st, n]
    return bass.AP(ap.tensor, off, pat)


def phase_s5a(k, T, UT_d, SIN_d):
    S = k.S
    CB = 32
    NB = NCH // CB
    with ExitStack() as es:
        ublk = [[k.sb(es, [128, 4, CB * TAU], BF16, "ublk") for _ in range(2)] for _ in range(2)]
        Zb = [k.sb(es, [64, CB, 2, 2, 32, 2], F32, "Zb") for _ in range(2)]
        carry = k.sb(es, [64, 2, 2, 32, 2], F32, "carry")
        m1 = k.sb(es, [64, 2, 2, 32, 2], F32, "m1")
        m2 = k.sb(es, [64, 2, 2, 32, 2], F32, "m2")
        Sb = [[k.sb(es, [128, 2, CB, 32], BF16, "Sb") for _ in range(2)] for _ in range(2)]
        zs = k.sb(es, [128, 32], BF16, "zs")
        S.dve(lambda e: e.memset(zs[:, :], 0.0), w=["zs"])
        S.dve(lambda e: e.memset(carry[:, :, :, :, :], 0.0), w=["carry"])
        for s in range(NSEQ):
            S.dma("pool", SIN_d[s][0][:, 0, :], zs[:, :], r=["zs"], stream="st")
            S.dma("pool", SIN_d[s][1][:, CTX // TAU - 1, :], zs[:, :], r=["zs"], stream="st")
        order = {0: list(range(NB)), 1: [0] + list(range(NB - 1, 0, -1))}

        def stage1(step):
            par = step % 2
            Z = Zb[par]
            ztok = ("Zb", par)
            for d in range(2):
                B = order[d][step]
                for s in range(NSEQ):
                    S.dma("sp", ublk[d][s][:, :, :], UT_d[s][:, :, B * CB * TAU:(B + 1) * CB * TAU], w=[("ublk", d, s)])
                for s in range(NSEQ):
                    bq = [k.bank() for _ in range(4)]
                    for slot in range(8):
                        q, pr = slot // 2, slot % 2
                        for i in range(TAU):
                            for qd in range(4):
                                S.pe(lambda e, bk=bq[qd], slot=slot, q=q, qd=qd, pr=pr, i=i, d=d, s=s: e.matmul(
                                    k.ps[bk][:, slot * CB:(slot + 1) * CB], T["Wup"][32 * qd:32 * qd + 32, pr, d * 4 + q, i, :],
                                    ublk[d][s][32 * qd:32 * qd + 32, q, i:CB * TAU:TAU], start=(i == 0), stop=(i == TAU - 1),
                                    tile_position=(32 * qd, 0), skip_group_check=True),
                                    r=["Wup", ("ublk", d, s)], w=[("ps", bq[qd])])
                    for qd in range(4):
                        for ri in range(2):
                            src = k.ps[bq[qd]][ri * 64:(ri + 1) * 64, 0:8 * CB].rearrange("p (q r c) -> p q r c", q=4, r=2)
                            if d == 1:
                                src = rev_axis(src, 3)
                            S.act(lambda e, qd=qd, ri=ri, s=s, Z=Z, d=d, src=src: e.activation(
                                out=Z[:, :, ri, d, :, s].rearrange("p c (q m) -> p q m c", m=8)[:, :, 2 * qd:2 * qd + 2, :],
                                in_=src, func=AF.Copy), r=[("ps", bq[qd])], w=[(ztok, d)])

        stage1(0)
        for step in range(NB):
            par = step % 2
            Z = Zb[par]
            ztok = ("Zb", par)
            if step + 1 < NB:
                stage1(step + 1)
            zr = [(ztok, 0), (ztok, 1)]
            prev = carry[:, :, :, :, :]
            ptok = ["carry"]
            for kk in range(CB):
                cur = Z[:, kk, :, :, :, :]
                S.dve(lambda e, prev=prev: e.tensor_tensor(out=m1[:, :, :, :, :], in0=prev, in1=T["LRm"][:, :, :, :, :], op=ALU.mult),
                      r=ptok + ["LRm"], w=["m1"])
                S.dve(lambda e, prev=prev: e.tensor_tensor(out=m2[:, :, :, :, :], in0=rev_axis(prev, 1), in1=T["LIm"][:, :, :, :, :], op=ALU.mult),
                      r=ptok + ["LIm"], w=["m2"])
                S.dve(lambda e, cur=cur: e.tensor_tensor(out=m1[:, :, :, :, :], in0=m1[:, :, :, :, :], in1=cur, op=ALU.add),
                      r=["m1"] + zr, w=["m1"])
                S.dve(lambda e, cur=cur: e.tensor_tensor(out=cur, in0=m1[:, :, :, :, :], in1=m2[:, :, :, :, :], op=ALU.add),
                      r=["m1", "m2"], w=zr)
                prev = cur
                ptok = zr
            S.dve(lambda e, prev=prev: e.tensor_copy(out=carry[:, :, :, :, :], in_=prev), r=zr, w=["carry"])
            for d in range(2):
                B = order[d][step]
                sb_ = Sb[d][par]
                stok = ("Sb", d, par)
                for ri in range(2):
                    src = Z[:, :, ri, d, :, :].rearrange("p c g s -> p s c g")
                    if d == 1:
                        src = rev_axis(src, 2)
                    S.act(lambda e, ri=ri, sb_=sb_, src=src: e.activation(out=sb_[ri * 64:(ri + 1) * 64, :, :, :], in_=src, func=AF.Copy),
                          r=zr, w=[stok])
                c0 = B * CB
                for s in range(NSEQ):
                    if d == 0:
                        n = CB if B < NB - 1 else CB - 1
                        S.dma("pool", SIN_d[s][0][:, c0 + 1:c0 + 1 + n, :], sb_[:, s, 0:n, :], r=[stok], stream="st")
                    else:
                        lo = 1 if B <= 1 else 0
                        S.dma("pool", SIN_d[s][1][:, c0 + lo - 1:c0 + CB - 1, :], sb_[:, s, lo:CB, :], r=[stok], stream="st")
                        if B == 0:
                            S.dma("pool", SIN_d[s][1][:, NCH - 1, :], sb_[:, s, 0, :], r=[stok], stream="st")
        S.barrier()


def phase_s5b(k, T, UT_d, SIN_d, ST_d, p):
    S = k.S
    with ExitStack() as es:
        wglu = k.sb(es, [128, 4, 512], BF16, "wglu")
        cols = load_cols(k, es, [p["ssm_d"], p["b_glu"]], "s5cols")
        ublk = [k.sb(es, [128, 4, 512], BF16, "ublk") for _ in range(2)]
        sin = [[k.sb(es, [128, 64, 32], BF16, "sin") for _ in range(2)] for _ in range(2)]
        yf = [k.sb(es, [128, 512], F32, "yf") for _ in range(2)]
        yg = [k.sb(es, [128, 4, 512], BF16, "yg") for _ in range(2)]
        sgl = [k.sb(es, [128, 512], F32, "sgl") for _ in range(2)]
        so = [k.sb(es, [128, 4, 512], BF16, "so") for _ in range(2)]
        for q in range(4):
            S.dma("pool", wglu[:, q, :], p["w_glu"][q * 128:(q + 1) * 128, :], w=[("wglu", q)], stream="ldw")
        bi = 0
        ny = 0
        for s in range(NSEQ):
            for (t0, Tn, isctx) in seq_blocks(512):
                NC = Tn // TAU
                c0 = t0 // TAU
                par = bi % 2
                u = ublk[par]
                utok = ("ublk", par)
                S.dma("sp", u[:, :, 0:Tn], UT_d[s][:, :, t0:t0 + Tn], w=[utok])
                for d in range(2):
                    S.dma("sp", sin[par][d][:, 0:NC, :], SIN_d[s][d][:, c0:c0 + NC, :], w=[("sin", par, d)])
                ygt = yg[par]
                for q in range(4):
                    bk = k.bank()
                    for j in range(TAU):
                        out = k.ps[bk][:, j:Tn:TAU]
                        for i in range(TAU):
                            slots = []
                            if i <= j:
                                slots.append(j - i)
                            if i >= j:
                                slots.append(8 + i - j)
                            for sl in slots:
                                first = (i == 0 and sl == slots[0])
                                S.pe(lambda e, out=out, q=q, sl=sl, i=i, u=u, Tn=Tn, first=first: e.matmul(
                                    out, T["Knear"][:, q, sl, :], u[:, q, i:Tn:TAU], start=first, stop=False, skip_group_check=True),
                                    r=["Knear", utok], w=[("ps", bk)])
                        for qd in range(4):
                            for pr in range(2):
                                g = q * 8 + qd * 2 + pr
                                for d in range(2):
                                    last = (pr == 1 and d == 1)
                                    S.pe(lambda e, bk=bk, j=j, qd=qd, g=g, d=d, Tn=Tn, NC=NC, last=last, par=par: e.matmul(
                                        k.ps[bk][32 * qd:32 * qd + 32, j:Tn:TAU], T["Wfar"][:, d * 32 + g, j, :],
                                        sin[par][d][:, 0:NC, g], start=False, stop=last, skip_group_check=True,
                                        tile_position=(0, 32 * qd)),
                                        r=["Wfar", ("sin", par, d)], w=[("ps", bk)])
                    y_ = yf[ny % 2]
                    S.dve(lambda e, bk=bk, q=q, u=u, Tn=Tn, y_=y_: e.scalar_tensor_tensor(
                        out=y_[:, 0:Tn], in0=u[:, q, 0:Tn], scalar=cols[:, 0, q:q + 1], in1=k.ps[bk][:, 0:Tn],
                        op0=ALU.mult, op1=ALU.add), r=[("ps", bk), utok, "s5cols"], w=[("yf", ny % 2)])
                    S.act(lambda e, q=q, Tn=Tn, y_=y_, ygt=ygt: e.activation(out=ygt[:, q, 0:Tn], in_=y_[:, 0:Tn], func=AF.Gelu_apprx_tanh),
                          r=[("yf", ny % 2)], w=[("yg", par, q)])
                    ny += 1
                so_ = so[par]
                for m in range(4):
                    bk = k.bank()
                    for q in range(4):
                        S.pe(lambda e, bk=bk, q=q, m=m, Tn=Tn, ygt=ygt: e.matmul(
                            k.ps[bk][:, 0:Tn], wglu[:, q, m * 128:(m + 1) * 128], ygt[:, q, 0:Tn], start=(q == 0), stop=(q == 3)),
                            r=[("wglu", q), ("yg", par, q)], w=[("ps", bk)])
                    sg_ = sgl[m % 2]
                    S.act(lambda e, bk=bk, m=m, Tn=Tn, sg_=sg_: e.activation(out=sg_[:, 0:Tn], in_=k.ps[bk][:, 0:Tn], func=AF.Sigmoid,
                                                                           bias=cols[:, 1, m:m + 1], scale=1.0),
                          r=[("ps", bk), "s5cols"], w=[("sgl", m % 2)])
                    S.dve(lambda e, m=m, Tn=Tn, sg_=sg_, ygt=ygt, so_=so_: e.tensor_tensor(
                        out=so_[:, m, 0:Tn], in0=ygt[:, m, 0:Tn], in1=sg_[:, 0:Tn], op=ALU.mult),
                        r=[("yg", par, m), ("sgl", m % 2)], w=[("so", par, m)])
                S.dma("pool", ST_d[s][:, :, t0:t0 + Tn], so_[:, :, 0:Tn], r=[("so", par, m) for m in range(4)], stream="st")
                bi += 1
        S.barrier()


HD = 128
NH = 8
NKV = 2
SM_SCALE = HD ** -0.5
DEN_DVE_EVERY = 0


def qk_norm_rope(k, ps_src, nh, gbc, gtok, rc, rs, tile_i, dst, dtok, ss, sstok, tq, tqtok, rope):
    S = k.S
    for h in range(nh):
        src, srctok = ps_src[h]
        S.act(lambda e, src=src, h=h: e.activation(out=k.junk[:, h, :], in_=src, func=AF.Square, accum_out=ss[:, h:h + 1]),
              r=[srctok], w=[(sstok, h), ("junk", h)])
    S.act(lambda e: e.activation(out=ss[:, 8:8 + nh], in_=ss[:, 0:nh], func=AF.Ln, bias=RMS_EPS, scale=1.0 / HD),
          r=[(sstok, h) for h in range(nh)], w=[(sstok, "m")])
    S.act(lambda e: e.activation(out=ss[:, 16:16 + nh], in_=ss[:, 8:8 + nh], func=AF.Exp, scale=-0.5),
          r=[(sstok, "m")], w=[(sstok, "r")])
    tgt = tq if rope else dst
    wt = [tqtok, (tqtok, 1), (tqtok, 2)] if rope else [(dtok, 0), (dtok, 1)]
    for h in range(nh):
        src, srctok = ps_src[h]
        S.dve(lambda e, src=src, h=h: e.scalar_tensor_tensor(out=tgt[:, h * 128:(h + 1) * 128], in0=src, scalar=ss[:, 16 + h:17 + h],
                                                             in1=gbc[:, :], op0=ALU.mult, op1=ALU.mult),
              r=[srctok, (sstok, "r"), gtok], w=wt)
    if not rope:
        return
    x3 = tq[:, 0:nh * 128].rearrange("p (h d) -> p h d", d=128)
    o3 = dst[:, 0:nh * 128].rearrange("p (h d) -> p h d", d=128)
    c3 = bc(rc[:, tile_i, :].unsqueeze(1), [128, nh, 64])
    s3 = bc(rs[:, tile_i, :].unsqueeze(1), [128, nh, 64])
    x1, x2 = x3[:, :, 0:64], x3[:, :, 64:128]
    S.dve(lambda e: e.tensor_tensor(out=o3[:, :, 0:64], in0=x1, in1=c3, op=ALU.mult), r=[tqtok, "rope"], w=[(dtok, 0)])
    S.dve(lambda e: e.tensor_tensor(out=o3[:, :, 64:128], in0=x2, in1=c3, op=ALU.mult), r=[tqtok, "rope"], w=[(dtok, 1)])
    S.dve(lambda e: e.tensor_tensor(out=x2, in0=x2, in1=s3, op=ALU.mult), r=[tqtok, (dtok, 1), "rope"], w=[(tqtok, 2)])
    S.dve(lambda e: e.tensor_tensor(out=x1, in0=x1, in1=s3, op=ALU.mult), r=[tqtok, (dtok, 0), "rope"], w=[(tqtok, 1)])
    S.dve(lambda e: e.tensor_tensor(out=o3[:, :, 0:64], in0=o3[:, :, 0:64], in1=x2, op=ALU.subtract), r=[(dtok, 0), (tqtok, 2)], w=[(dtok, 0)])
    S.dve(lambda e: e.tensor_tensor(out=o3[:, :, 64:128], in0=o3[:, :, 64:128], in1=x1, op=ALU.add), r=[(dtok, 1), (tqtok, 1)], w=[(dtok, 1)])


def phase_l1a(k, X2_d, wc_d, kg_d, KT_d, V_d, rope_d):
    S = k.S
    with ExitStack() as es:
        wkv = k.sb(es, [128, 8, 512], BF16, "wkv")
        kg = k.sb(es, [128, HD], F32, "kg")
        rc = k.sb(es, [128, 32, 64], F32, "rc")
        rs = k.sb(es, [128, 32, 64], F32, "rs")
        xin = [k.sb(es, [128, 4, D], F32, "xin") for _ in range(2)]
        hT = [k.sb(es, [128, 8, 512], BF16, "hT") for _ in range(2)]
        kf = [k.sb(es, [128, 256], F32, "kf") for _ in range(8)]
        tq = [k.sb(es, [128, 256], F32, "tq") for _ in range(8)]
        ss = [k.sb(es, [128, 24], F32, "ss") for _ in range(8)]
        vb = [k.sb(es, [128, 256], BF16, "vb") for _ in range(8)]
        kTs = [k.sb(es, [128, 2, 512], BF16, "kTs") for _ in range(2)]
        for kc in range(8):
            S.dma("pool", wkv[:, kc, :], wc_d[kc * 128:(kc + 1) * 128, 1024:1536], w=[("wkv", kc)], stream="ldw")
        load_rows_bc(k, kg, kg_d, "kg")
        S.dma("sp", rc[:, :, :], rope_d[0].rearrange("(t p) i -> p t i", p=128), w=["rope"])
        S.dma("sp", rs[:, :, :], rope_d[1].rearrange("(t p) i -> p t i", p=128), w=["rope"])
        blks = [(b, t0, T, isctx) for b in range(NSEQ) for (t0, T, isctx) in seq_blocks(512)]

        def stage_a(bi):
            b, t0, T, isctx = blks[bi]
            row = 2 if isctx else b
            ntile = T // 128
            x, xt = xin[bi % 2], ("xin", bi % 2)
            h, ht = hT[bi % 2], ("hT", bi % 2)
            for tt in range(ntile):
                S.dma("sp", x[:, tt, :], X2_d[b][t0 + tt * 128:t0 + (tt + 1) * 128, :], w=[(xt, tt)])
            mod_transpose(k, x, xt, ntile, h, ht, 1, 8, 0, row)
            for tt in range(ntile):
                u = (bi % 2) * 4 + tt
                bk = k.bank()
                for kc in range(8):
                    S.pe(lambda e, bk=bk, kc=kc, tt=tt, h=h: e.matmul(k.ps[bk][:, :], h[:, kc, tt * 128:(tt + 1) * 128], wkv[:, kc, :],
                                                                      start=(kc == 0), stop=(kc == 7)),
                         r=[("wkv", kc), (ht, kc)], w=[("ps", bk)])
                v_ = vb[u]
                S.act(lambda e, bk=bk, v_=v_: e.activation(out=v_[:, :], in_=k.ps[bk][:, 256:512], func=AF.Copy),
                      r=[("ps", bk)], w=[("vb", u)])
                S.dma("pool", V_d[b][t0 + tt * 128:t0 + (tt + 1) * 128, :], v_[:, :], r=[("vb", u)], stream="st")
                srcs = [(k.ps[bk][:, hh * 128:(hh + 1) * 128], ("ps", bk)) for hh in range(2)]
                ti = (t0 - CTX) // 128 + tt if not isctx else 0
                qk_norm_rope(k, srcs, 2, kg, "kg", rc, rs, ti, kf[u], ("kf", u), ss[u], ("ss", u),
                             tq[u], ("tq", u), rope=not isctx)

        def stage_b(bi):
            b, t0, T, isctx = blks[bi]
            ntile = T // 128
            kt_ = kTs[bi % 2]
            for tt in range(ntile):
                u = (bi % 2) * 4 + tt
                kft = [(("kf", u), 0), (("kf", u), 1)]
                for hh in range(2):
                    b2 = k.bank()
                    S.pe(lambda e, b2=b2, hh=hh, u=u: e.transpose(k.ps[b2][:, 0:128], kf[u][:, hh * 128:(hh + 1) * 128], k.ident[:, :]),
                         r=kft + ["ident"], w=[("ps", b2)])
                    S.act(lambda e, b2=b2, hh=hh, tt=tt, kt_=kt_: e.activation(out=kt_[:, hh, tt * 128:(tt + 1) * 128], in_=k.ps[b2][:, 0:128], func=AF.Copy),
                          r=[("ps", b2)], w=[("kTs", bi % 2, tt)])
            S.dma("pool", KT_d[b][:, :, t0:t0 + T], kt_[:, :, 0:T], r=[("kTs", bi % 2, tt) for tt in range(ntile)], stream="st")

        stage_a(0)
        for bi in range(len(blks)):
            if bi + 1 < len(blks):
                stage_a(bi + 1)
            stage_b(bi)
        S.barrier()


def phase_l1b(k, X2_d, wc_d, qg_d, wo_d, KT_d, V_d, rope_d, lng_d, lnb_d, grow_d, X3_d):
    S = k.S
    NKT = NTOK // 128
    with ExitStack() as es:
        wq = k.sb(es, [128, 8, D], BF16, "wq")
        wo = k.sb(es, [128, 8, D], BF16, "wo")
        qg = k.sb(es, [128, HD], F32, "qg")
        rc = k.sb(es, [128, 32, 64], F32, "rc")
        rs = k.sb(es, [128, 32, 64], F32, "rs")
        lng = k.sb(es, [128, D], F32, "lng")
        lnb = k.sb(es, [128, D], F32, "lnb")
        gate = k.sb(es, [128, D], F32, "gate")
        KT = k.sb(es, [128, 2, NTOK], BF16, "KT")
        V = k.sb(es, [128, NKT, 256], BF16, "V")
        xin = [k.sb(es, [128, 4, D], F32, "xin") for _ in range(2)]
        hT = k.sb(es, [128, 8, 512], BF16, "hT")
        qf = [k.sb(es, [128, D], F32, "qf") for _ in range(4)]
        tq = k.sb(es, [128, D], F32, "tq")
        ss = [k.sb(es, [128, 24], F32, "ss") for _ in range(2)]
        QT = [k.sb(es, [128, 8, 512], BF16, "QT") for _ in range(2)]
        pT = [k.sb(es, [128, 512], BF16, "pT") for _ in range(4)]
        rden = k.sb(es, [128, 512], F32, "rden")
        dacc = k.sb(es, [128, 512], F32, "dacc")
        OT = [k.sb(es, [128, 8, 512], BF16, "OT") for _ in range(2)]
        tmp = k.sb(es, [128, D], F32, "tmp")
        st = [k.sb(es, [128, 20], F32, "st") for _ in range(2)]
        for kc in range(8):
            S.dma("pool", wq[:, kc, :], wc_d[kc * 128:(kc + 1) * 128, 0:1024], w=[("wq", kc)], stream="ldw")
            S.dma("pool", wo[:, kc, :], wo_d[kc * 128:(kc + 1) * 128, :], w=[("wo", kc)], stream="ldw")
        load_rows_bc(k, qg, qg_d, "qg")
        load_rows_bc(k, lng, lng_d, "lng")
        load_rows_bc(k, lnb, lnb_d, "lnb")
        S.dma("sp", rc[:, :, :], rope_d[0].rearrange("(t p) i -> p t i", p=128), w=["rope"])
        S.dma("sp", rs[:, :, :], rope_d[1].rearrange("(t p) i -> p t i", p=128), w=["rope"])
        blks = [(b, qb) for b in range(NSEQ) for qb in range(SEQ // 512)]
        cnt = {"pt": 0, "qk": 0, "gate": None, "kv": None}

        def load_x(i):
            b, qb = blks[i]
            t0 = CTX + qb * 512
            x, xt = xin[i % 2], ("xin", i % 2)
            for tt in range(4):
                S.dma("sp", x[:, tt, :], X2_d[b][t0 + tt * 128:t0 + (tt + 1) * 128, :], w=[(xt, tt)])

        def prep_a(i):
            b, qb = blks[i]
            x, xt = xin[i % 2], ("xin", i % 2)
            mod_transpose(k, x, xt, 4, hT, "hT", 1, 8, 0, b)
            for tt in range(4):
                qb_ = (k.bank(), k.bank())
                for hh in range(2):
                    for kc in range(8):
                        S.pe(lambda e, hh=hh, kc=kc, tt=tt, qb_=qb_: e.matmul(
                            k.ps[qb_[hh]][:, :], hT[:, kc, tt * 128:(tt + 1) * 128], wq[:, kc, hh * 512:(hh + 1) * 512],
                            start=(kc == 0), stop=(kc == 7)), r=[("wq", kc), ("hT", kc)], w=[("ps", qb_[hh])])
                for hh in range(2):
                    S.act(lambda e, hh=hh, tt=tt, qb_=qb_: e.activation(out=qf[tt][:, hh * 512:(hh + 1) * 512], in_=k.ps[qb_[hh]][:, :], func=AF.Copy),
                          r=[("ps", qb_[hh])], w=[("qfraw", tt, hh), (("qf", tt), 0), (("qf", tt), 1)])
                srcs = [(qf[tt][:, h * 128:(h + 1) * 128], ("qfraw", tt, h // 4)) for h in range(8)]
                qk_norm_rope(k, srcs, 8, qg, "qg", rc, rs, qb * 4 + tt, qf[tt], ("qf", tt), ss[tt % 2], ("ss", tt % 2),
                             tq, "tq", rope=True)

        def prep_b(i):
            Q = QT[i % 2]
            for tt in range(4):
                for h in range(8):
                    b2 = k.bank()
                    S.pe(lambda e, b2=b2, h=h, tt=tt: e.transpose(k.ps[b2][:, 0:128], qf[tt][:, h * 128:(h + 1) * 128], k.ident[:, :]),
                         r=[(("qf", tt), 0), (("qf", tt), 1), "ident"], w=[("ps", b2)])
                    if h % 2 == 0:
                        S.act(lambda e, b2=b2, h=h, tt=tt, Q=Q: e.activation(out=Q[:, h, tt * 128:(tt + 1) * 128], in_=k.ps[b2][:, 0:128], func=AF.Copy),
                              r=[("ps", b2)], w=[("QT", i % 2, h)])
                    else:
                        S.dve(lambda e, b2=b2, h=h, tt=tt, Q=Q: e.tensor_copy(out=Q[:, h, tt * 128:(tt + 1) * 128], in_=k.ps[b2][:, 0:128]),
                              r=[("ps", b2)], w=[("QT", i % 2, h)])

        def head(i, h):
            b, qb = blks[i]
            if cnt["kv"] != b:
                cnt["kv"] = b
                S.dma("sp", KT[:, :, :], KT_d[b], w=["KT"])
                S.dma("sp", V[:, :, :], V_d[b].rearrange("(t p) c -> p t c", p=128), w=["V"])
            Q = QT[i % 2]
            O = OT[i % 2]
            kvh = h // 4
            bo, bd = (0, 1) if h % 2 == 0 else (2, 3)
            pend = []

            def issue_pv(item):
                kt, p_, ptok = item
                S.pe(lambda e: e.matmul(k.ps[bo][:, :], V[:, kt, kvh * 128:(kvh + 1) * 128], p_[:, :],
                                        start=(kt == 0), stop=(kt == NKT - 1)), r=["V", ptok], w=[("ps", bo)])
                if DEN_DVE_EVERY and kt % DEN_DVE_EVERY == 1:
                    if kt == 1:
                        S.dve(lambda e: e.tensor_copy(out=dacc[:, :], in_=p_[:, :]), r=[ptok], w=["dacc"])
                    else:
                        S.dve(lambda e: e.tensor_tensor(out=dacc[:, :], in0=dacc[:, :], in1=p_[:, :], op=ALU.add), r=[ptok, "dacc"], w=["dacc"])
                else:
                    S.pe(lambda e: e.matmul(k.ps[bd][:, :], k.ones_b[:, :], p_[:, :], start=(kt == 0), stop=False),
                         r=["ones_b", ptok], w=[("ps", bd)])
            for kt in range(NKT):
                bs = 4 + cnt["qk"] % 4
                cnt["qk"] += 1
                S.pe(lambda e, bs=bs, kt=kt: e.matmul(k.ps[bs][:, :], KT[:, kvh, kt * 128:(kt + 1) * 128], Q[:, h, :],
                                                      start=True, stop=True), r=["KT", ("QT", i % 2, h)], w=[("ps", bs)])
                p_ = pT[cnt["pt"] % 4]
                ptok = ("pT", cnt["pt"] % 4)
                cnt["pt"] += 1
                S.act(lambda e, bs=bs, p_=p_: e.activation(out=p_[:, :], in_=k.ps[bs][:, :], func=AF.Exp, scale=SM_SCALE),
                      r=[("ps", bs)], w=[ptok])
                pend.append((kt, p_, ptok))
                if len(pend) > 2:
                    issue_pv(pend.pop(0))
            while pend:
                issue_pv(pend.pop(0))
            if DEN_DVE_EVERY:
                S.pe(lambda e: e.matmul(k.ps[bd][:, :], k.ones_f[:, :], dacc[:, :], start=False, stop=True),
                     r=["ones_f", "dacc"], w=[("ps", bd)])
            S.dve(lambda e: e.reciprocal(out=rden[:, :], in_=k.ps[bd][:, :]), r=[("ps", bd)], w=["rden"])
            S.dve(lambda e: e.tensor_tensor(out=O[:, h, :], in0=k.ps[bo][:, :], in1=rden[:, :], op=ALU.mult),
                  r=[("ps", bo), "rden"], w=[("OT", i % 2, h)])

        def post(i):
            b, qb = blks[i]
            if cnt["gate"] != b:
                cnt["gate"] = b
                load_rows_bc(k, gate, grow_d[1, 0, b], "gate")
            x, xt = xin[i % 2], ("xin", i % 2)
            O = OT[i % 2]
            for tt in range(4):
                yb = (k.bank(), k.bank())
                for hh in range(2):
                    for h in range(8):
                        S.pe(lambda e, hh=hh, h=h, tt=tt, yb=yb: e.matmul(
                            k.ps[yb[hh]][:, :], O[:, h, tt * 128:(tt + 1) * 128], wo[:, h, hh * 512:(hh + 1) * 512],
                            start=(h == 0), stop=(h == 7)), r=[("OT", i % 2, h), ("wo", h)], w=[("ps", yb[hh])])
                ln_epilogue(k, yb, x[:, tt, :], (xt, tt), gate, "gate", lng, lnb, tmp, "tmp",
                            st[tt % 2], ("st", tt % 2), x[:, tt, :], (xt, tt))
                S.dma("pool", X3_d[b][qb * 512 + tt * 128:qb * 512 + (tt + 1) * 128, :], x[:, tt, :], r=[(xt, tt)], stream="st")

        n = len(blks)
        load_x(0)
        prep_a(0)
        prep_b(0)
        load_x(1)
        for i in range(n):
            head(i, 0)
            head(i, 1)
            if i > 0:
                post(i - 1)
                if i + 1 < n:
                    load_x(i + 1)
            head(i, 2)
            head(i, 3)
            if i + 1 < n:
                prep_a(i + 1)
            head(i, 4)
            head(i, 5)
            head(i, 6)
            if i + 1 < n:
                prep_b(i + 1)
            head(i, 7)
        post(n - 1)
        S.barrier()


IN_SPECS = [
    ("x", [NSEQ, SEQ, D]), ("c", [NSEQ, D]), ("ctx", [NSEQ, CTX, D]), ("c_ctx", [D]),
    ("w_mod", [2, D, 6 * D]), ("b_mod", [2, 6 * D]), ("ln_g", [2, 2, D]), ("ln_b", [2, 2, D]),
    ("w_in_ab", [1, D, 1536]), ("conv_w", [1, 31, 512]), ("conv_b", [1, 512]), ("conv_ln_g", [1, 512]),
    ("conv_ln_b", [1, 512]), ("ssm_lambda_re", [1, 2, 32, 64]), ("ssm_lambda_im", [1, 2, 32, 64]),
    ("ssm_log_dt", [1, 2, 32]), ("ssm_b_re", [1, 2, 32, 64, 16]), ("ssm_b_im", [1, 2, 32, 64, 16]),
    ("ssm_c_re", [1, 2, 32, 16, 64]), ("ssm_c_im", [1, 2, 32, 16, 64]), ("ssm_d", [1, 512]),
    ("ssm_w_glu", [1, 512, 512]), ("ssm_b_glu", [1, 512]), ("w_out_ab", [1, D, D]), ("w_in_c", [1, D, 1536]),
    ("q_norm_g", [1, HD]), ("k_norm_g", [1, HD]), ("w_out_c", [1, D, D]), ("w_ffn_in", [2, D, 2 * DFF]),
    ("w_ffn_out", [2, DFF, D]),
    ("cst_bdmask", [128, 128]), ("cst_pmask", [128, 2]), ("cst_rope", [2, SEQ, 64]),
]


def host_consts():
    r = np.arange(128)
    bd = (r[:, None] // 16 == r[None, :] // 16).astype(np.float32)
    pm = np.stack([((r // 16) % 2 == 0), ((r // 16) % 2 == 1)], 1).astype(np.float32)
    pos = np.arange(SEQ)
    freqs = (10000.0 ** (-np.arange(0, 64, 2, dtype=np.float64) / 64.0))
    ang = np.concatenate([(pos // 64)[:, None] * freqs[None, :], (pos % 64)[:, None] * freqs[None, :]], 1)
    rope = np.stack([np.cos(ang), np.sin(ang)], 0).astype(np.float32)
    return {"cst_bdmask": bd, "cst_pmask": pm, "cst_rope": rope}


def s5_tables(k, es, d):
    T = {
        "Wup": k.sb(es, [128, 2, 8, TAU, 128], BF16, "Wup"),
        "Wfar": k.sb(es, [128, 64, TAU, 32], BF16, "Wfar"),
        "Knear": k.sb(es, [128, 4, 16, 128], BF16, "Knear"),
        "LRm": k.sb(es, [64, 2, 2, 32, 2], F32, "LRm"),
        "LIm": k.sb(es, [64, 2, 2, 32, 2], F32, "LIm"),
        "bdmask": k.sb(es, [128, 128], F32, "bdmask"),
        "pmask": k.sb(es, [128, 2], F32, "pmask"),
    }
    k.S.dma("sp", T["bdmask"][:, :], d["cst_bdmask"], w=["bdmask"])
    k.S.dma("sp", T["pmask"][:, :], d["cst_pmask"], w=["pmask"])
    return T


def s5_params(d):
    return {"lam_re": d["ssm_lambda_re"][0], "lam_im": d["ssm_lambda_im"][0], "log_dt": d["ssm_log_dt"][0],
            "b_re": d["ssm_b_re"][0], "b_im": d["ssm_b_im"][0], "c_re": d["ssm_c_re"][0], "c_im": d["ssm_c_im"][0],
            "ssm_d": d["ssm_d"][0], "w_glu": d["ssm_w_glu"][0], "b_glu": d["ssm_b_glu"][0]}


def build_program(debug=False):
    nc = bass.Bass("TRN2", target_bir_lowering=False)
    d = {n: nc.dram_tensor(n, list(s), F32, kind="ExternalInput").ap() for n, s in IN_SPECS}
    skind = "ExternalOutput" if debug else "Internal"

    def scratch(name, shape, dt):
        return nc.dram_tensor(name, list(shape), dt, kind=skind).ap()
    out_d = nc.dram_tensor("out", [NSEQ, SEQ, D], F32, kind="ExternalOutput").ap()
    grow = scratch("s_grow", [2, 2, 3, D], F32)
    AT = scratch("s_AT", [NSEQ, 128, 4, A_LEN], BF16)
    UT = scratch("s_UT", [NSEQ, 128, 4, NTOK], BF16)
    CT = scratch("s_CT", [NSEQ, 128, 4, NTOK], BF16)
    ST = scratch("s_ST", [NSEQ, 128, 4, NTOK], BF16)
    SIN = scratch("s_SIN", [NSEQ, 2, 128, NCH, 32], BF16)
    X1 = scratch("s_X1", [NSEQ, NTOK, D], F32)
    X2 = scratch("s_X2", [NSEQ, NTOK, D], F32)
    X3 = scratch("s_X3", [NSEQ, SEQ, D], F32)
    KT = scratch("s_KT", [NSEQ, 128, 2, NTOK], BF16)
    VV = scratch("s_V", [NSEQ, NTOK, 256], BF16)
    k = K(nc)
    phase_mod(k, d["c"], d["c_ctx"], d["w_mod"], d["b_mod"], grow)
    phase_l0a(k, d["x"], d["ctx"], d["w_in_ab"][0], AT, UT)
    phase_conv(k, AT, CT, d["conv_w"][0], d["conv_b"][0], d["conv_ln_g"][0], d["conv_ln_b"][0])
    with ExitStack() as es:
        T = s5_tables(k, es, d)
        p = s5_params(d)
        phase_s5prep(k, T, p)
        phase_s5a(k, T, UT, SIN)
        phase_s5b(k, T, UT, SIN, ST, p)
    blocks = []
    for b in range(NSEQ):
        for (t0, Tn, isctx) in seq_blocks(512):
            blocks.append(([CT[b][:, :, t0:t0 + Tn], ST[b][:, :, t0:t0 + Tn]],
                           d["ctx"][b, t0:t0 + Tn, :] if isctx else d["x"][b, t0 - CTX:t0 - CTX + Tn, :],
                           X1[b][t0:t0 + Tn, :], 2 if isctx else b))
    phase_outproj(k, 0, None, d["w_out_ab"][0], d["ln_g"][0, 0], d["ln_b"][0, 0], grow, blocks)
    fb = []
    for b in range(NSEQ):
        for (t0, Tn, isctx) in seq_blocks(256):
            fb.append((X1[b][t0:t0 + Tn, :], X2[b][t0:t0 + Tn, :], 2 if isctx else b))
    phase_ffn(k, 0, fb, d["w_ffn_in"][0], d["w_ffn_out"][0], d["ln_g"][0, 1], d["ln_b"][0, 1], grow)
    phase_l1a(k, X2, d["w_in_c"][0], d["k_norm_g"][0], KT, VV, d["cst_rope"])
    phase_l1b(k, X2, d["w_in_c"][0], d["q_norm_g"][0], d["w_out_c"][0], KT, VV, d["cst_rope"],
              d["ln_g"][1, 0], d["ln_b"][1, 0], grow, X3)
    fb = []
    for b in range(NSEQ):
        for t0 in range(0, SEQ, 256):
            fb.append((X3[b][t0:t0 + 256, :], out_d[b][t0:t0 + 256, :], b))
    phase_ffn(k, 1, fb, d["w_ffn_in"][1], d["w_ffn_out"][1], d["ln_g"][1, 1], d["ln_b"][1, 1], grow)
    k.S.emit(final_waits=["st"])
    return nc


_PROGRAM = None


def kernel(**inputs):
    global _PROGRAM
    if _PROGRAM is None:
        _PROGRAM = build_program()
    nc = _PROGRAM
    cst = host_consts()
    in_maps = []
    for core in range(8):
        m = {}
        sl = slice(core * NSEQ, (core + 1) * NSEQ)
        for n, _ in IN_SPECS:
            if n.startswith("cst_"):
                m[n] = cst[n]
            elif n in ("x", "c", "ctx"):
                m[n] = np.ascontiguousarray(np.asarray(inputs[n], dtype=np.float32)[sl])
            else:
                m[n] = np.ascontiguousarray(np.asarray(inputs[n], dtype=np.float32))
        in_maps.append(m)
    res = run_bass_kernel_spmd(nc, in_maps, core_ids=list(range(8)))
    return np.concatenate([np.asarray(r["out"], dtype=np.float32) for r in res.results], axis=0)
```

```python
import math
from contextlib import ExitStack
import numpy as np
import concourse.bass as bass
import concourse.mybir as mybir
from concourse.bass_utils import run_bass_kernel_spmd

F32 = mybir.dt.float32
BF16 = mybir.dt.bfloat16
I32 = mybir.dt.int32
AF = mybir.ActivationFunctionType
ALU = mybir.AluOpType
AX = mybir.AxisListType

D = 1024
SEQ = 4096
CTX = 256
NTOK = SEQ + CTX
DFF = 2816
ALPHA = 4.0 ** 0.25
LN_EPS = 1e-5
RMS_EPS = 1e-6
NSEQ = 2

ENGS = ("pe", "act", "dve", "pool", "sp")
NSEM = 8


class Op:
    __slots__ = ("eng", "fn", "deps", "flag", "count", "stream", "scount", "sidx")

    def __init__(self, eng, fn, stream):
        self.eng = eng
        self.fn = fn
        self.deps = ()
        self.flag = False
        self.count = 0
        self.stream = stream
        self.scount = 0
        self.sidx = 0


class Sched:
    def __init__(self, nc):
        self.nc = nc
        self.q = {e: [] for e in ENGS}
        self.last_w = {}
        self.readers = {}
        self.streams = {}
        self.last_stream_op = {}

    def add(self, eng, fn, r=(), w=(), stream=None, extra=()):
        op = Op(eng, fn, stream)
        deps = set(extra)
        for t in r:
            lw = self.last_w.get(t)
            if lw is not None:
                deps.add(lw)
        for t in w:
            lw = self.last_w.get(t)
            if lw is not None:
                deps.add(lw)
            for rd in self.readers.get(t, ()):
                deps.add(rd)
        for t in r:
            self.readers.setdefault(t, []).append(op)
        for t in w:
            self.last_w[t] = op
            self.readers[t] = []
        deps.discard(op)
        if stream is not None:
            n = self.streams.get(stream, 0)
            self.streams[stream] = n + 1
            op.sidx = n % NSEM
            op.scount = n // NSEM + 1
            self.last_stream_op[(stream, op.sidx)] = op
        op.deps = deps
        self.q[eng].append(op)
        return op

    def pe(self, fn, r=(), w=()):
        return self.add("pe", fn, r, w)

    def act(self, fn, r=(), w=()):
        return self.add("act", fn, r, w)

    def dve(self, fn, r=(), w=()):
        return self.add("dve", fn, r, w)

    def pool(self, fn, r=(), w=()):
        return self.add("pool", fn, r, w)

    def dma(self, eng, out, in_, r=(), w=(), stream="ld", **kw):
        def fn(e, out=out, in_=in_, kw=kw):
            return e.dma_start(out=out, in_=in_, **kw)
        return self.add(eng, fn, r, w, stream=stream)

    def barrier(self):
        lasts = []
        for e in ("pe", "act", "dve", "pool"):
            for o in reversed(self.q[e]):
                if o.fn is not None and o.stream is None:
                    lasts.append(o)
                    break
        lasts += list(self.last_stream_op.values())
        for e in ENGS:
            self.add(e, None, extra=lasts)
        self.last_w = {}
        self.readers = {}

    def emit(self, final_waits=()):
        nc = self.nc
        for e in ENGS:
            for op in self.q[e]:
                for p in op.deps:
                    if p.stream is not None:
                        continue
                    if p.eng == "pe" and op.eng == "pe" and op.stream is None and op.fn is not None:
                        continue
                    p.flag = True
        for e in ENGS:
            c = 0
            for op in self.q[e]:
                if op.stream is None and op.flag:
                    c += 1
                    op.count = c
        sems = {e: nc.alloc_semaphore("sem_" + e) for e in ("pe", "act", "dve", "pool")}
        ssem = {(s, i): nc.alloc_semaphore("dma_%s%d" % (s, i)) for s in self.streams for i in range(NSEM)}
        engmap = {"pe": "tensor", "act": "scalar", "dve": "vector", "pool": "gpsimd", "sp": "sync"}
        with nc.Block() as block:
            for e in ENGS:
                ops = self.q[e]

                def body(eng, ops=ops, e=e):
                    waited = {}
                    for op in ops:
                        need = {}
                        for p in op.deps:
                            if p.stream is not None:
                                key = ("s", p.stream, p.sidx)
                                val = 16 * p.scount
                            else:
                                if (p.eng == "pe" and e == "pe" and op.stream is None
                                        and op.fn is not None):
                                    continue
                                key = ("e", p.eng)
                                val = p.count
                            if val > need.get(key, 0):
                                need[key] = val
                        if op.stream is not None and op.scount > 1:
                            key = ("s", op.stream, op.sidx)
                            val = 16 * (op.scount - 1)
                            if val > need.get(key, 0):
                                need[key] = val
                        for key, val in need.items():
                            if waited.get(key, 0) >= val:
                                continue
                            waited[key] = val
                            sem = ssem[key[1:]] if key[0] == "s" else sems[key[1]]
                            eng.wait_ge(sem, val)
                        if op.fn is None:
                            continue
                        ins = op.fn(eng)
                        if op.stream is not None:
                            ins.then_inc(ssem[(op.stream, op.sidx)], 16)
                        elif op.flag:
                            ins.then_inc(sems[e], 1)
                    if e == "sp":
                        for s in final_waits:
                            n = self.streams[s]
                            for i in range(min(NSEM, n)):
                                eng.wait_ge(ssem[(s, i)], 16 * ((n - 1 - i) // NSEM + 1))

                getattr(block, engmap[e])(body)


class K:
    def __init__(self, nc):
        self.nc = nc
        self.S = Sched(nc)
        self.ps = [nc.alloc_psum_tensor("ps%d" % i, [128, 512], F32) for i in range(8)]
        self.psn = 0
        self.uid = 0
        S = self.S
        self.ident = nc.alloc_sbuf_tensor("ident", [128, 128], F32)
        self.ones_f = nc.alloc_sbuf_tensor("ones_f", [128, 128], F32)
        self.ones_b = nc.alloc_sbuf_tensor("ones_b", [128, 128], BF16)
        self.mhalf = nc.alloc_sbuf_tensor("mhalf", [128, 1], F32)
        self.junk = nc.alloc_sbuf_tensor("junk", [128, 8, 128], BF16)
        self.modT = [nc.alloc_sbuf_tensor("modT%d" % l, [128, 48, 3], F32) for l in range(2)]
        ident = self.ident

        S.pool(lambda e: e.memset(ident[:], 0.0), w=["ident"])
        S.pool(lambda e: e.affine_select(out=ident[:], in_=ident[:], pattern=[[-1, 128]],
                                         compare_op=ALU.not_equal, fill=1.0, base=0, channel_multiplier=1),
               r=["ident"], w=["ident"])
        S.pool(lambda e: e.memset(self.ones_f[:], 1.0), w=["ones_f"])
        S.pool(lambda e: e.memset(self.ones_b[:], 1.0), w=["ones_b"])
        S.pool(lambda e: e.memset(self.mhalf[:], -0.5), w=["mhalf"])

    def bank(self):
        i = self.psn % 8
        self.psn += 1
        return i

    def sb(self, es, shape, dtype, name="t"):
        self.uid += 1
        return es.enter_context(self.nc.sbuf_tensor("%s_%d" % (name, self.uid), list(shape), dtype))


def mod_transpose(k, xin, xtok, ntile, hT, htok, l, sc_chunk0, sh_chunk0, row):
    S = k.S
    mT = k.modT[l]
    for kc in range(8):
        b = k.bank()
        for tt in range(ntile):
            S.pe(lambda e, b=b, tt=tt, kc=kc: e.transpose(
                k.ps[b][:, tt * 128:(tt + 1) * 128], xin[:, tt, kc * 128:(kc + 1) * 128], k.ident[:]),
                r=[(xtok, tt), "ident"], w=[("ps", b)])
        S.act(lambda e, b=b, kc=kc: e.activation(
            out=hT[:, kc, 0:ntile * 128], in_=k.ps[b][:, 0:ntile * 128], func=AF.Identity,
            bias=mT[:, sh_chunk0 + kc, row:row + 1], scale=mT[:, sc_chunk0 + kc, row:row + 1]),
            r=[("ps", b), ("modT", l)], w=[(htok, kc)])


def ln_epilogue(k, ybanks, xres, xtok, gate, gtok, lng, lnb, tmp, ttok, st, sttok, out_ap, otok, via_act=False,
                gb="pool", defer=False):
    S = k.S
    for h in range(2):
        if via_act:
            S.act(lambda e, h=h: e.activation(out=tmp[:, h * 512:(h + 1) * 512], in_=k.ps[ybanks[h]][:, :], func=AF.Copy),
                  r=[("ps", ybanks[h])], w=[(ttok, h)])
            S.pool(lambda e, h=h: e.tensor_tensor(out=tmp[:, h * 512:(h + 1) * 512], in0=tmp[:, h * 512:(h + 1) * 512],
                                                  in1=gate[:, h * 512:(h + 1) * 512], op=ALU.mult),
                   r=[(ttok, h), gtok], w=[(ttok, h)])
        else:
            S.dve(lambda e, h=h: e.tensor_tensor(out=tmp[:, h * 512:(h + 1) * 512], in0=k.ps[ybanks[h]][:, :],
                                                 in1=gate[:, h * 512:(h + 1) * 512], op=ALU.mult),
                  r=[("ps", ybanks[h]), gtok], w=[(ttok, h)])
    S.dve(lambda e: e.scalar_tensor_tensor(out=out_ap, in0=xres, scalar=ALPHA, in1=tmp[:, :],
                                           op0=ALU.mult, op1=ALU.add),
          r=[xtok, (ttok, 0), (ttok, 1)], w=[otok])
    for h in range(2):
        S.dve(lambda e, h=h: e.bn_stats(out=st[:, h * 6:(h + 1) * 6], in_=out_ap[:, h * 512:(h + 1) * 512]),
              r=[otok], w=[(sttok, h)])
    S.dve(lambda e: e.bn_aggr(out=st[:, 12:14], in_=st[:, 0:12]), r=[(sttok, 0), (sttok, 1)], w=[(sttok, 2)])
    S.dve(lambda e: e.tensor_scalar(out=st[:, 14:15], in0=st[:, 13:14], scalar1=LN_EPS, scalar2=None, op0=ALU.add),
          r=[(sttok, 2)], w=[(sttok, 3)])
    S.pool(lambda e: e.tensor_tensor(out=st[:, 15:16], in0=st[:, 14:15], in1=k.mhalf[:, 0:1], op=ALU.pow),
           r=[(sttok, 3), "mhalf"], w=[(sttok, 4)])
    S.dve(lambda e: e.scalar_tensor_tensor(out=st[:, 16:17], in0=st[:, 12:13], scalar=-1.0, in1=st[:, 15:16],
                                           op0=ALU.mult, op1=ALU.mult),
          r=[(sttok, 2), (sttok, 4)], w=[(sttok, 5)])
    S.act(lambda e: e.activation(out=out_ap, in_=out_ap, func=AF.Identity, bias=st[:, 16:17], scale=st[:, 15:16]),
          r=[otok, (sttok, 4), (sttok, 5)], w=[otok])
    def fin():
        S.add(gb, lambda e: e.tensor_tensor(out=out_ap, in0=out_ap, in1=lng[:, :], op=ALU.mult), r=[otok, "lng"], w=[otok])
        S.add(gb, lambda e: e.tensor_tensor(out=out_ap, in0=out_ap, in1=lnb[:, :], op=ALU.add), r=[otok, "lnb"], w=[otok])
    if defer:
        return fin
    fin()


def load_rows_bc(k, tile, dram_row, tok, eng="sp"):
    k.S.dma(eng, tile[:, :], dram_row.partition_broadcast(128), w=[tok])


def phase_mod(k, c_d, cctx_d, wmod_d, bmod_d, grow_d):
    S = k.S
    nc = k.nc
    with ExitStack() as es:
        c3 = k.sb(es, [3, D], F32, "c3")
        s3 = k.sb(es, [3, D], F32, "s3")
        scT = k.sb(es, [128, 8, 3], F32, "scT")
        b48 = k.sb(es, [48, 128], F32, "b48")
        bT = k.sb(es, [128, 48], F32, "bT")
        bias3 = k.sb(es, [3, 2, D], F32, "bias3")
        grow = k.sb(es, [3, D], F32, "grow")
        wsl = [k.sb(es, [128, 8, D], F32, "wsl") for _ in range(2)]
        S.dma("sp", c3[0:2, :], c_d, w=["c3"])
        S.dma("sp", c3[2:3, :], cctx_d.rearrange("(o n) -> o n", o=1), w=["c3"])
        S.act(lambda e: e.activation(out=s3[:, :], in_=c3[:, :], func=AF.Silu), r=["c3"], w=["s3"])
        b = k.bank()
        for kc in range(8):
            S.pe(lambda e, kc=kc, b=b: e.transpose(k.ps[b][:, kc * 3:(kc + 1) * 3], s3[0:3, kc * 128:(kc + 1) * 128],
                                                   k.ident[0:3, 0:3]), r=["s3", "ident"], w=[("ps", b)])
        S.dve(lambda e, b=b: e.tensor_copy(out=scT[:, :, :], in_=k.ps[b][:, 0:24]), r=[("ps", b)], w=["scT"])
        nsl = 0
        for l in range(2):
            S.dma("sp", b48[:, :], bmod_d[l].rearrange("(j p) -> j p", p=128), r=[], w=["b48"])
            b = k.bank()
            S.pe(lambda e, b=b: e.transpose(k.ps[b][:, 0:48], b48[0:48, :], k.ident[0:48, 0:48]),
                 r=["b48", "ident"], w=[("ps", b)])
            S.dve(lambda e, b=b: e.tensor_copy(out=bT[:, :], in_=k.ps[b][:, 0:48]), r=[("ps", b)], w=["bT"])
            for gi, s in enumerate((2, 5)):
                S.dma("sp", bias3[:, gi, :], bmod_d[l, s * D:(s + 1) * D].partition_broadcast(3), w=[("bias3", gi)])
            for s in range(6):
                wt = wsl[nsl % 2]
                wtok = ("wsl", nsl % 2)
                nsl += 1
                S.dma("sp", wt[:, :, :], wmod_d[l, :, s * D:(s + 1) * D].rearrange("(k p) n -> p k n", p=128), w=[wtok])
                b = k.bank()
                for j in range(8):
                    for kc in range(8):
                        S.pe(lambda e, b=b, j=j, kc=kc, wt=wt: e.matmul(
                            k.ps[b][:, j * 3:(j + 1) * 3], wt[:, kc, j * 128:(j + 1) * 128], scT[:, kc, :],
                            start=(kc == 0), stop=(kc == 7)), r=[wtok, "scT"], w=[("ps", b)])
                for r_ in range(3):
                    S.dve(lambda e, b=b, r_=r_, s=s, l=l: e.tensor_tensor(
                        out=k.modT[l][:, s * 8:(s + 1) * 8, r_], in0=k.ps[b][:, r_:24:3], in1=bT[:, s * 8:(s + 1) * 8],
                        op=ALU.add), r=[("ps", b), "bT"], w=[("modT", l)])
                if s in (2, 5):
                    gi = 0 if s == 2 else 1
                    for h in range(2):
                        b2 = k.bank()
                        for kc in range(8):
                            S.pe(lambda e, b2=b2, kc=kc, h=h, wt=wt: e.matmul(
                                k.ps[b2][0:3, :], scT[:, kc, :], wt[:, kc, h * 512:(h + 1) * 512],
                                start=(kc == 0), stop=(kc == 7)), r=[wtok, "scT"], w=[("ps", b2)])
                        S.dve(lambda e, b2=b2, h=h, gi=gi: e.tensor_tensor(
                            out=grow[:, h * 512:(h + 1) * 512], in0=k.ps[b2][0:3, :],
                            in1=bias3[:, gi, h * 512:(h + 1) * 512], op=ALU.add),
                            r=[("ps", b2), ("bias3", gi)], w=[("grow", h)])
                    S.dma("pool", grow_d[l, gi], grow[:, :], r=[("grow", 0), ("grow", 1)], stream="st")
            for c0 in (8, 32):
                S.dve(lambda e, l=l, c0=c0: e.tensor_scalar(
                    out=k.modT[l][:, c0:c0 + 8, :], in0=k.modT[l][:, c0:c0 + 8, :], scalar1=1.0, scalar2=None,
                    op0=ALU.add), r=[("modT", l)], w=[("modT", l)])
        S.barrier()


def phase_ffn(k, l, blocks, w1_d, w2_d, lng_d, lnb_d, grow_d):
    S = k.S
    with ExitStack() as es:
        w1 = k.sb(es, [128, 8, 2 * DFF], BF16, "w1")
        w2 = k.sb(es, [128, 22, D], BF16, "w2")
        lng = k.sb(es, [128, D], F32, "lng")
        lnb = k.sb(es, [128, D], F32, "lnb")
        gate = [k.sb(es, [128, D], F32, "gate") for _ in range(2)]
        xin = [k.sb(es, [128, 2, D], F32, "xin") for _ in range(2)]
        hT = [k.sb(es, [128, 8, 256], BF16, "hT") for _ in range(2)]
        actT = k.sb(es, [128, 22, 256], BF16, "actT")
        sg = [k.sb(es, [128, 256], BF16, "sg") for _ in range(2)]
        tmp = [k.sb(es, [128, D], F32, "tmp") for _ in range(2)]
        st = [k.sb(es, [128, 20], F32, "st") for _ in range(2)]
        for kc in range(8):
            S.dma("pool", w1[:, kc, :], w1_d[kc * 128:(kc + 1) * 128, :], w=[("w1", kc)], stream="ldw")
        for j in range(22):
            S.dma("pool", w2[:, j, :], w2_d[j * 128:(j + 1) * 128, :], w=[("w2", j)], stream="ldw")
        load_rows_bc(k, lng, lng_d, "lng")
        load_rows_bc(k, lnb, lnb_d, "lnb")
        cur_row = None
        ng = 0
        nt = 0
        def front(bi):
            in_ap, out_ap, row = blocks[bi]
            ntile = in_ap.shape[0] // 128
            x, xt = xin[bi % 2], ("xin", bi % 2)
            for tt in range(ntile):
                S.dma("sp", x[:, tt, :], in_ap[tt * 128:(tt + 1) * 128, :], w=[(xt, tt)])
            mod_transpose(k, x, xt, ntile, hT[bi % 2], ("hT", bi % 2), l, 32, 24, row)

        front(0)
        for bi, (in_ap, out_ap, row) in enumerate(blocks):
            T = in_ap.shape[0]
            ntile = T // 128
            if row != cur_row:
                g = gate[ng % 2]
                gtok = ("gate", ng % 2)
                ng += 1
                load_rows_bc(k, g, grow_d[l, 1, row], gtok)
                cur_row = row
            x = xin[bi % 2]
            xt = ("xin", bi % 2)
            h = hT[bi % 2]
            ht = ("hT", bi % 2)
            for j in range(22):
                bg = k.bank()
                bu = k.bank()
                for (bb, col) in ((bg, j * 128), (bu, DFF + j * 128)):
                    for kc in range(8):
                        S.pe(lambda e, bb=bb, col=col, kc=kc, h=h, T=T: e.matmul(
                            k.ps[bb][:, 0:T], w1[:, kc, col:col + 128], h[:, kc, 0:T],
                            start=(kc == 0), stop=(kc == 7)), r=[("w1", kc), (ht, kc)], w=[("ps", bb)])
                s_ = sg[j % 2]
                S.act(lambda e, bg=bg, s_=s_, T=T: e.activation(out=s_[:, 0:T], in_=k.ps[bg][:, 0:T], func=AF.Silu),
                      r=[("ps", bg)], w=[("sg", j % 2)])
                S.dve(lambda e, bu=bu, s_=s_, j=j, T=T: e.tensor_tensor(
                    out=actT[:, j, 0:T], in0=k.ps[bu][:, 0:T], in1=s_[:, 0:T], op=ALU.mult),
                    r=[("ps", bu), ("sg", j % 2)], w=[("actT", j)])
            if bi + 1 < len(blocks):
                front(bi + 1)
            for tt in range(ntile):
                yb = (k.bank(), k.bank())
                for hh in range(2):
                    for j in range(22):
                        S.pe(lambda e, hh=hh, j=j, tt=tt, yb=yb: e.matmul(
                            k.ps[yb[hh]][:, :], actT[:, j, tt * 128:(tt + 1) * 128], w2[:, j, hh * 512:(hh + 1) * 512],
                            start=(j == 0), stop=(j == 21)), r=[("actT", j), ("w2", j)], w=[("ps", yb[hh])])
                ln_epilogue(k, yb, x[:, tt, :], (xt, tt), g, gtok, lng, lnb, tmp[nt % 2], ("tmp", nt % 2),
                            st[nt % 2], ("st", nt % 2), x[:, tt, :], (xt, tt))
                nt += 1
                S.dma("pool", out_ap[tt * 128:(tt + 1) * 128, :], x[:, tt, :], r=[(xt, tt)], stream="st")
        S.barrier()


APAD = 15
A_CTX0 = APAD
A_LAT0 = APAD + CTX + 2 * APAD
A_LEN = A_LAT0 + SEQ + APAD


def seq_blocks(T):
    out = [(0, CTX, True)] if T >= CTX else [(t, T, True) for t in range(0, CTX, T)]
    out += [(CTX + t, T, False) for t in range(0, SEQ, T)]
    return out


def tok_src(x_d, ctx_d, b, t0, n):
    if t0 < CTX:
        return ctx_d[b, t0:t0 + n, :]
    return x_d[b, t0 - CTX:t0 - CTX + n, :]


def phase_l0a(k, x_d, ctx_d, wab_d, AT_d, UT_d):
    S = k.S
    with ExitStack() as es:
        wab = k.sb(es, [128, 8, 1536], BF16, "wab")
        xin = [k.sb(es, [128, 4, D], F32, "xin") for _ in range(2)]
        hT = [k.sb(es, [128, 8, 512], BF16, "hT") for _ in range(2)]
        aTs = [k.sb(es, [128, 4, 512], BF16, "aTs") for _ in range(2)]
        uTs = [k.sb(es, [128, 4, 512], BF16, "uTs") for _ in range(2)]
        sg = [k.sb(es, [128, 512], BF16, "sg") for _ in range(2)]
        zt = k.sb(es, [128, 4, 2 * APAD], BF16, "zt")
        for kc in range(8):
            S.dma("pool", wab[:, kc, :], wab_d[kc * 128:(kc + 1) * 128, :], w=[("wab", kc)], stream="ldw")
        S.dve(lambda e: e.memset(zt[:, :, :], 0.0), w=["zt"])
        bi = 0
        for b in range(NSEQ):
            for (o, n) in ((0, APAD), (A_CTX0 + CTX, 2 * APAD), (A_LAT0 + SEQ, APAD)):
                S.dma("pool", AT_d[b][:, :, o:o + n], zt[:, :, 0:n], r=["zt"], stream="st")
            for (t0, T, isctx) in seq_blocks(512):
                row = 2 if isctx else b
                ntile = T // 128
                x = xin[bi % 2]
                xt = ("xin", bi % 2)
                h = hT[bi % 2]
                ht = ("hT", bi % 2)
                a_ = aTs[bi % 2]
                u_ = uTs[bi % 2]
                for tt in range(ntile):
                    S.dma("sp", x[:, tt, :], tok_src(x_d, ctx_d, b, t0 + tt * 128, 128), w=[(xt, tt)])
                mod_transpose(k, x, xt, ntile, h, ht, 0, 8, 0, row)
                for mc in (4, 0, 5, 1, 6, 2, 7, 3, 8, 9, 10, 11):
                    bk = k.bank()
                    for kc in range(8):
                        S.pe(lambda e, bk=bk, mc=mc, kc=kc, h=h, T=T: e.matmul(
                            k.ps[bk][:, 0:T], wab[:, kc, mc * 128:(mc + 1) * 128], h[:, kc, 0:T],
                            start=(kc == 0), stop=(kc == 7)), r=[("wab", kc), (ht, kc)], w=[("ps", bk)])
                    if 4 <= mc < 8:
                        s_ = sg[mc % 2]
                        S.act(lambda e, bk=bk, s_=s_, T=T: e.activation(out=s_[:, 0:T], in_=k.ps[bk][:, 0:T], func=AF.Sigmoid),
                              r=[("ps", bk)], w=[("sg", mc % 2)])
                    elif mc < 4:
                        s_ = sg[mc % 2]
                        S.dve(lambda e, bk=bk, s_=s_, T=T, mc=mc, a_=a_: e.tensor_tensor(
                            out=a_[:, mc, 0:T], in0=k.ps[bk][:, 0:T], in1=s_[:, 0:T], op=ALU.mult),
                            r=[("ps", bk), ("sg", mc % 2)], w=[("aTs", bi % 2, mc)])
                    else:
                        S.act(lambda e, bk=bk, T=T, mc=mc, u_=u_: e.activation(
                            out=u_[:, mc - 8, 0:T], in_=k.ps[bk][:, 0:T], func=AF.Copy),
                            r=[("ps", bk)], w=[("uTs", bi % 2, mc - 8)])
                ao = (A_CTX0 + t0) if isctx else (A_LAT0 + t0 - CTX)
                S.dma("pool", AT_d[b][:, :, ao:ao + T], a_[:, :, 0:T], r=[("aTs", bi % 2, q) for q in range(4)], stream="st")
                S.dma("pool", UT_d[b][:, :, t0:t0 + T], u_[:, :, 0:T], r=[("uTs", bi % 2, q) for q in range(4)], stream="st")
                bi += 1
        S.barrier()


def load_cols(k, es, rows_aps, name):
    S = k.S
    nr = len(rows_aps)
    q = rows_aps[0].shape[0] // 128
    rt = k.sb(es, [nr * q, 128], F32, name + "r")
    ct = k.sb(es, [128, nr, q], F32, name)
    for i, ap in enumerate(rows_aps):
        S.dma("sp", rt[i * q:(i + 1) * q, :], ap.rearrange("(j p) -> j p", p=128), w=[(name, "r", i)])
    b = k.bank()
    S.pe(lambda e: e.transpose(k.ps[b][:, 0:nr * q], rt[0:nr * q, :], k.ident[0:nr * q, 0:nr * q]),
         r=[(name, "r", i) for i in range(nr)] + ["ident"], w=[("ps", b)])
    S.dve(lambda e: e.tensor_copy(out=ct[:, :, :], in_=k.ps[b][:, 0:nr * q]), r=[("ps", b)], w=[name])
    return ct


def phase_conv(k, AT_d, CT_d, convw_d, convb_d, cg_d, cb_d):
    S = k.S
    with ExitStack() as es:
        cols = load_cols(k, es, [convb_d, cg_d, cb_d], "ccols")
        cw31 = k.sb(es, [31, 512], F32, "cw31")
        cwT = k.sb(es, [128, 4, 31], F32, "cwT")
        dg = k.sb(es, [128, 4, 31, 128], BF16, "dg")
        ain = [k.sb(es, [128, 4, 512 + 2 * APAD], BF16, "ain") for _ in range(2)]
        xc = k.sb(es, [128, 4, 512], F32, "xc")
        xsq = k.sb(es, [128, 4, 512], F32, "xsq")
        mean = k.sb(es, [128, 512], F32, "mean")
        rstd = k.sb(es, [128, 512], F32, "rstd")
        cs = [k.sb(es, [128, 4, 512], BF16, "cs") for _ in range(2)]
        S.dma("sp", cw31[:, :], convw_d, w=["cw31"])
        for q in range(4):
            b = k.bank()
            S.pe(lambda e, q=q, b=b: e.transpose(k.ps[b][:, 0:31], cw31[0:31, q * 128:(q + 1) * 128], k.ident[0:31, 0:31]),
                 r=["cw31", "ident"], w=[("ps", b)])
            S.dve(lambda e, q=q, b=b: e.tensor_copy(out=cwT[:, q, :], in_=k.ps[b][:, 0:31]), r=[("ps", b)], w=["cwT"])
        for q in range(4):
            for t in range(31):
                eng = S.dve if (t % 2 == 0) else S.pool
                eng(lambda e, q=q, t=t: e.tensor_scalar(out=dg[:, q, t, :], in0=k.ident[:, :], scalar1=cwT[:, q, t:t + 1],
                                                        scalar2=None, op0=ALU.mult), r=["cwT", "ident"], w=[("dg", q)])
        bi = 0
        for b_ in range(NSEQ):
            for (t0, T, isctx) in seq_blocks(512):
                a = ain[bi % 2]
                at = ("ain", bi % 2)
                c_ = cs[bi % 2]
                ao = (A_CTX0 + t0) if isctx else (A_LAT0 + t0 - CTX)
                S.dma("sp", a[:, :, 0:T + 2 * APAD], AT_d[b_][:, :, ao - APAD:ao + T + APAD], w=[at])
                for q in range(4):
                    bk = k.bank()
                    for t in range(31):
                        S.pe(lambda e, bk=bk, q=q, t=t, a=a, T=T: e.matmul(
                            k.ps[bk][:, 0:T], dg[:, q, t, :], a[:, q, t:t + T], start=(t == 0), stop=(t == 30)),
                            r=[("dg", q), at], w=[("ps", bk)])
                    S.act(lambda e, bk=bk, q=q, T=T: e.activation(out=xc[:, q, 0:T], in_=k.ps[bk][:, 0:T], func=AF.Identity,
                                                                  bias=cols[:, 0, q:q + 1], scale=1.0),
                          r=[("ps", bk), "ccols"], w=[("xc", q)])
                    S.act(lambda e, q=q, T=T: e.activation(out=xsq[:, q, 0:T], in_=xc[:, q, 0:T], func=AF.Square),
                          r=[("xc", q)], w=[("xsq", q)])
                b1 = k.bank()
                b2 = k.bank()
                for q in range(4):
                    S.pe(lambda e, q=q, b1=b1, T=T: e.matmul(k.ps[b1][:, 0:T], k.ones_f[:, :], xc[:, q, 0:T],
                                                             start=(q == 0), stop=(q == 3)),
                         r=["ones_f", ("xc", q)], w=[("ps", b1)])
                for q in range(4):
                    S.pe(lambda e, q=q, b2=b2, T=T: e.matmul(k.ps[b2][:, 0:T], k.ones_f[:, :], xsq[:, q, 0:T],
                                                             start=(q == 0), stop=(q == 3)),
                         r=["ones_f", ("xsq", q)], w=[("ps", b2)])
                S.act(lambda e, b1=b1, T=T: e.activation(out=mean[:, 0:T], in_=k.ps[b1][:, 0:T], func=AF.Copy, scale=1.0 / 512),
                      r=[("ps", b1)], w=["mean"])
                S.dve(lambda e, T=T: e.tensor_tensor(out=rstd[:, 0:T], in0=mean[:, 0:T], in1=mean[:, 0:T], op=ALU.mult),
                      r=["mean"], w=["rstd"])
                S.dve(lambda e, b2=b2, T=T: e.scalar_tensor_tensor(out=rstd[:, 0:T], in0=k.ps[b2][:, 0:T], scalar=1.0 / 512,
                                                                   in1=rstd[:, 0:T], op0=ALU.mult, op1=ALU.subtract),
                      r=[("ps", b2), "rstd"], w=["rstd"])
                S.act(lambda e, T=T: e.activation(out=rstd[:, 0:T], in_=rstd[:, 0:T], func=AF.Ln, bias=LN_EPS, scale=1.0),
                      r=["rstd"], w=["rstd"])
                S.act(lambda e, T=T: e.activation(out=rstd[:, 0:T], in_=rstd[:, 0:T], func=AF.Exp, scale=-0.5),
                      r=["rstd"], w=["rstd"])
                for q in range(4):
                    S.dve(lambda e, q=q, T=T: e.tensor_tensor(out=xc[:, q, 0:T], in0=xc[:, q, 0:T], in1=mean[:, 0:T], op=ALU.subtract),
                          r=[("xc", q), "mean"], w=[("xc", q)])
                    (S.pool if q % 2 else S.dve)(lambda e, q=q, T=T: e.tensor_tensor(out=xc[:, q, 0:T], in0=xc[:, q, 0:T], in1=rstd[:, 0:T], op=ALU.mult),
                                                 r=[("xc", q), "rstd"], w=[("xc", q)])
                    S.act(lambda e, q=q, T=T, c_=c_: e.activation(out=c_[:, q, 0:T], in_=xc[:, q, 0:T], func=AF.Silu,
                                                                  bias=cols[:, 2, q:q + 1], scale=cols[:, 1, q:q + 1]),
                          r=[("xc", q), "ccols"], w=[("cs", bi % 2, q)])
                S.dma("pool", CT_d[b_][:, :, t0:t0 + T], c_[:, :, 0:T], r=[("cs", bi % 2, q) for q in range(4)], stream="st")
                bi += 1
        S.barrier()


def phase_outproj(k, l, srcs, w_d, lng_d, lnb_d, grow_d, blocks):
    S = k.S
    with ExitStack() as es:
        wo = k.sb(es, [128, 8, D], BF16, "wo")
        lng = k.sb(es, [128, D], F32, "lng")
        lnb = k.sb(es, [128, D], F32, "lnb")
        gate = [k.sb(es, [128, D], F32, "gate") for _ in range(2)]
        src = [k.sb(es, [128, 8, 512], BF16, "src") for _ in range(2)]
        xin = [k.sb(es, [128, 4, D], F32, "xin") for _ in range(2)]
        tmp = [k.sb(es, [128, D], F32, "tmp") for _ in range(2)]
        st = [k.sb(es, [128, 20], F32, "st") for _ in range(2)]
        for kc in range(8):
            S.dma("pool", wo[:, kc, :], w_d[kc * 128:(kc + 1) * 128, :], w=[("wo", kc)], stream="ldw")
        load_rows_bc(k, lng, lng_d, "lng")
        load_rows_bc(k, lnb, lnb_d, "lnb")
        pending = None
        cur_row = None
        ng = 0
        nt = 0
        for bi, (src_aps, res_ap, out_ap, row) in enumerate(blocks):
            T = res_ap.shape[0]
            ntile = T // 128
            if row != cur_row:
                g = gate[ng % 2]
                gtok = ("gate", ng % 2)
                ng += 1
                load_rows_bc(k, g, grow_d[l, 0, row], gtok)
                cur_row = row
            s_ = src[bi % 2]
            stok = ("src", bi % 2)
            x = xin[bi % 2]
            xt = ("xin", bi % 2)
            c0 = 0
            for ap in src_aps:
                nch = ap.shape[1]
                S.dma("sp", s_[:, c0:c0 + nch, 0:T], ap, w=[(stok, c0)])
                c0 += nch
            srd = [(stok, c) for c in (0, 4)] if len(src_aps) == 2 else [(stok, 0)]
            for tt in range(ntile):
                S.dma("sp", x[:, tt, :], res_ap[tt * 128:(tt + 1) * 128, :], w=[(xt, tt)])
            for tt in range(ntile):
                yb = (k.bank(), k.bank())
                for hh in range(2):
                    for kc in range(8):
                        S.pe(lambda e, hh=hh, kc=kc, tt=tt, yb=yb, s_=s_: e.matmul(
                            k.ps[yb[hh]][:, :], s_[:, kc, tt * 128:(tt + 1) * 128], wo[:, kc, hh * 512:(hh + 1) * 512],
                            start=(kc == 0), stop=(kc == 7)), r=srd + [("wo", kc)], w=[("ps", yb[hh])])
                fin = ln_epilogue(k, yb, x[:, tt, :], (xt, tt), g, gtok, lng, lnb, tmp[nt % 2], ("tmp", nt % 2),
                                  st[nt % 2], ("st", nt % 2), x[:, tt, :], (xt, tt), gb="dve", defer=True)
                nt += 1
                if pending is not None:
                    pending()

                def pending(fin=fin, out_ap=out_ap, x=x, xt=xt, tt=tt):
                    fin()
                    S.dma("pool", out_ap[tt * 128:(tt + 1) * 128, :], x[:, tt, :], r=[(xt, tt)], stream="st")
        if pending is not None:
            pending()
        S.barrier()


TAU = 8
PREP_STAGE = 99
NCH = NTOK // TAU
MAGIC = 12582912.0
TWO_PI = 2.0 * math.pi


def bc(ap, shape):
    return ap.to_broadcast(list(shape))


def sincos(k, es, ang, n, tok):
    S = k.S
    outs = []
    for name, shift in (("sin", 0.0), ("cos", 0.5 * math.pi)):
        kk = k.sb(es, [128, n], F32, "kk" + name)
        rr = k.sb(es, [128, n], F32, "rr" + name)
        res = k.sb(es, [128, n], F32, "res" + name)
        S.dve(lambda e, kk=kk, shift=shift: e.tensor_scalar(out=kk[:, :], in0=ang, scalar1=1.0 / TWO_PI,
                                                            scalar2=shift / TWO_PI, op0=ALU.mult, op1=ALU.add),
              r=[tok], w=[(tok, name, "kk")])
        S.dve(lambda e, kk=kk: e.tensor_scalar(out=kk[:, :], in0=kk[:, :], scalar1=MAGIC, scalar2=None, op0=ALU.add),
              r=[(tok, name, "kk")], w=[(tok, name, "kk")])
        S.dve(lambda e, kk=kk: e.tensor_scalar(out=kk[:, :], in0=kk[:, :], scalar1=-MAGIC, scalar2=None, op0=ALU.add),
              r=[(tok, name, "kk")], w=[(tok, name, "kk")])
        S.dve(lambda e, kk=kk, rr=rr: e.scalar_tensor_tensor(out=rr[:, :], in0=kk[:, :], scalar=-TWO_PI, in1=ang,
                                                             op0=ALU.mult, op1=ALU.add),
              r=[(tok, name, "kk"), tok], w=[(tok, name, "rr")])
        S.dve(lambda e, rr=rr, shift=shift: e.tensor_scalar(out=rr[:, :], in0=rr[:, :], scalar1=shift, scalar2=None, op0=ALU.add),
              r=[(tok, name, "rr")], w=[(tok, name, "rr")])
        S.dve(lambda e, rr=rr: e.tensor_scalar(out=rr[:, :], in0=rr[:, :], scalar1=-3.14159, scalar2=3.14159,
                                               op0=ALU.max, op1=ALU.min),
              r=[(tok, name, "rr")], w=[(tok, name, "rr")])
        S.act(lambda e, rr=rr, res=res: e.activation(out=res[:, :], in_=rr[:, :], func=AF.Sin),
              r=[(tok, name, "rr")], w=[(tok, name)])
        outs.append(res)
    return outs


def phase_s5prep(k, T, p):
    S = k.S
    with ExitStack() as es:
        rows = k.sb(es, [64, 2, 2, 64], F32, "lrows")
        lamT = k.sb(es, [128, 2, 64], F32, "lamT")
        dtb = k.sb(es, [128, 64], F32, "dtb")
        ell = k.sb(es, [128, 64], F32, "ell")
        phi = k.sb(es, [128, 64], F32, "phi")
        kvec = k.sb(es, [128, 9], F32, "kvec")
        ang = k.sb(es, [128, 64, 9], F32, "ang")
        mag = k.sb(es, [128, 64, 9], F32, "mag")
        lre = k.sb(es, [128, 64, 9], F32, "lre")
        lim = k.sb(es, [128, 64, 9], F32, "lim")
        for ri, nm in enumerate(("lam_re", "lam_im")):
            for dup in range(2):
                S.dma("sp", rows[:, ri, dup, :], p[nm].rearrange("d g p -> (d g) p"), w=[("rows", ri, dup)])
        for ri in range(2):
            b = k.bank()
            S.pe(lambda e, ri=ri, b=b: e.transpose(k.ps[b][:, 0:64], rows[0:64, ri, :, :].rearrange("p a b -> p (a b)"), k.ident[0:64, 0:64]),
                 r=[("rows", ri, 0), ("rows", ri, 1), "ident"], w=[("ps", b)])
            S.dve(lambda e, ri=ri, b=b: e.tensor_copy(out=lamT[:, ri, :], in_=k.ps[b][:, 0:64]), r=[("ps", b)], w=["lamT"])
        S.dma("sp", dtb[:, :], p["log_dt"].rearrange("d g -> (d g)").partition_broadcast(128), w=["dtb"])
        S.act(lambda e: e.activation(out=dtb[:, :], in_=dtb[:, :], func=AF.Exp), r=["dtb"], w=["dtb"])
        S.dve(lambda e: e.tensor_tensor(out=ell[:, :], in0=lamT[:, 0, :], in1=dtb[:, :], op=ALU.mult), r=["lamT", "dtb"], w=["ell"])
        S.dve(lambda e: e.tensor_tensor(out=phi[:, :], in0=lamT[:, 1, :], in1=dtb[:, :], op=ALU.mult), r=["lamT", "dtb"], w=["phi"])
        for i in range(9):
            S.pool(lambda e, i=i: e.memset(kvec[:, i:i + 1], float(i)), w=["kvec"])
        S.dve(lambda e: e.tensor_tensor(out=ang[:, :, :], in0=bc(phi[:, :].unsqueeze(2), [128, 64, 9]),
                                        in1=bc(kvec[:, :].unsqueeze(1), [128, 64, 9]), op=ALU.mult),
              r=["phi", "kvec"], w=["ang"])
        S.dve(lambda e: e.tensor_tensor(out=mag[:, :, :], in0=bc(ell[:, :].unsqueeze(2), [128, 64, 9]),
                                        in1=bc(kvec[:, :].unsqueeze(1), [128, 64, 9]), op=ALU.mult),
              r=["ell", "kvec"], w=["mag"])
        S.act(lambda e: e.activation(out=mag[:, :, :], in_=mag[:, :, :], func=AF.Exp), r=["mag"], w=["mag"])
        sn, cs = sincos(k, es, ang[:, :, :].rearrange("p a b -> p (a b)"), 576, "ang")
        S.dve(lambda e: e.tensor_tensor(out=lre[:, :, :].rearrange("p a b -> p (a b)"), in0=mag[:, :, :].rearrange("p a b -> p (a b)"),
                                        in1=cs[:, :], op=ALU.mult), r=["mag", ("ang", "cos")], w=["lre"])
        S.dve(lambda e: e.tensor_tensor(out=lim[:, :, :].rearrange("p a b -> p (a b)"), in0=mag[:, :, :].rearrange("p a b -> p (a b)"),
                                        in1=sn[:, :], op=ALU.mult), r=["mag", ("ang", "sin")], w=["lim"])
        if PREP_STAGE <= 1:
            S.barrier()
            return
        for d in range(2):
            S.dve(lambda e, d=d: e.tensor_copy(out=T["LRm"][:, :, d, :, :],
                                               in_=bc(lre[0:64, d * 32:(d + 1) * 32, 8:9].unsqueeze(1), [64, 2, 32, 2])),
                  r=["lre"], w=["LRm"])
            S.dve(lambda e, d=d: e.tensor_scalar(out=T["LIm"][:, 0, d, :, :], in0=bc(lim[0:64, d * 32:(d + 1) * 32, 8:9], [64, 32, 2]),
                                                 scalar1=-1.0, scalar2=None, op0=ALU.mult), r=["lim"], w=["LIm"])
            S.dve(lambda e, d=d: e.tensor_copy(out=T["LIm"][:, 1, d, :, :], in_=bc(lim[0:64, d * 32:(d + 1) * 32, 8:9], [64, 32, 2])),
                  r=["lim"], w=["LIm"])
        nre = k.sb(es, [128, 64], F32, "nre")
        den = k.sb(es, [128, 64], F32, "den")
        t1 = k.sb(es, [128, 64], F32, "t1")
        kre = k.sb(es, [128, 64], F32, "kre")
        kim = k.sb(es, [128, 64], F32, "kim")
        L0, L1 = lamT[:, 0, :], lamT[:, 1, :]
        S.dve(lambda e: e.tensor_scalar(out=nre[:, :], in0=lre[:, :, 1], scalar1=-1.0, scalar2=None, op0=ALU.add), r=["lre"], w=["nre"])
        S.dve(lambda e: e.tensor_tensor(out=den[:, :], in0=L0, in1=L0, op=ALU.mult), r=["lamT"], w=["den"])
        S.dve(lambda e: e.tensor_tensor(out=t1[:, :], in0=L1, in1=L1, op=ALU.mult), r=["lamT"], w=["t1"])
        S.dve(lambda e: e.tensor_tensor(out=den[:, :], in0=den[:, :], in1=t1[:, :], op=ALU.add), r=["den", "t1"], w=["den"])
        S.dve(lambda e: e.reciprocal(out=den[:, :], in_=den[:, :]), r=["den"], w=["den"])
        S.dve(lambda e: e.tensor_tensor(out=kre[:, :], in0=nre[:, :], in1=L0, op=ALU.mult), r=["nre", "lamT"], w=["kre"])
        S.dve(lambda e: e.tensor_tensor(out=t1[:, :], in0=lim[:, :, 1], in1=L1, op=ALU.mult), r=["lim", "lamT", "den"], w=["t1"])
        S.dve(lambda e: e.tensor_tensor(out=kre[:, :], in0=kre[:, :], in1=t1[:, :], op=ALU.add), r=["kre", "t1"], w=["kre"])
        S.dve(lambda e: e.tensor_tensor(out=kre[:, :], in0=kre[:, :], in1=den[:, :], op=ALU.mult), r=["kre", "den"], w=["kre"])
        S.dve(lambda e: e.tensor_tensor(out=kim[:, :], in0=lim[:, :, 1], in1=L0, op=ALU.mult), r=["lim", "lamT"], w=["kim"])
        S.dve(lambda e: e.tensor_tensor(out=t1[:, :], in0=nre[:, :], in1=L1, op=ALU.mult), r=["nre", "lamT", "kre"], w=["t1"])
        S.dve(lambda e: e.tensor_tensor(out=kim[:, :], in0=kim[:, :], in1=t1[:, :], op=ALU.subtract), r=["kim", "t1"], w=["kim"])
        S.dve(lambda e: e.tensor_tensor(out=kim[:, :], in0=kim[:, :], in1=den[:, :], op=ALU.mult), r=["kim", "den"], w=["kim"])
        Y = k.sb(es, [128, 2, 64, 16], F32, "Y")
        X = k.sb(es, [128, 2, 64, 16], F32, "X")
        es1 = ExitStack()
        bp = k.sb(es1, [128, 2, 64, 16], F32, "bp")
        bb = k.sb(es1, [128, 2, 64, 16], F32, "bb")
        tb = k.sb(es1, [128, 64, 16], F32, "tb")
        for ri, nm in enumerate(("b_re", "b_im")):
            for half in range(2):
                for d in range(2):
                    S.dma("sp", bp[half * 64:(half + 1) * 64, ri, d * 32:(d + 1) * 32, :],
                          p[nm][d].rearrange("g p h -> p g h"), w=[("bp", ri)])
        kre3 = bc(kre[:, :].unsqueeze(2), [128, 64, 16])
        kim3 = bc(kim[:, :].unsqueeze(2), [128, 64, 16])
        S.dve(lambda e: e.tensor_tensor(out=bb[:, 0, :, :], in0=bp[:, 0, :, :], in1=kre3, op=ALU.mult), r=[("bp", 0), "kre"], w=[("bb", 0)])
        S.dve(lambda e: e.tensor_tensor(out=tb[:, :, :], in0=bp[:, 1, :, :], in1=kim3, op=ALU.mult), r=[("bp", 1), "kim"], w=["tb"])
        S.dve(lambda e: e.tensor_tensor(out=bb[:, 0, :, :], in0=bb[:, 0, :, :], in1=tb[:, :, :], op=ALU.subtract), r=[("bb", 0), "tb"], w=[("bb", 0)])
        S.dve(lambda e: e.tensor_tensor(out=bb[:, 1, :, :], in0=bp[:, 1, :, :], in1=kre3, op=ALU.mult), r=[("bp", 1), "kre"], w=[("bb", 1)])
        S.dve(lambda e: e.tensor_tensor(out=tb[:, :, :], in0=bp[:, 0, :, :], in1=kim3, op=ALU.mult), r=[("bp", 0), "kim", ("bb", 0)], w=["tb"])
        S.dve(lambda e: e.tensor_tensor(out=bb[:, 1, :, :], in0=bb[:, 1, :, :], in1=tb[:, :, :], op=ALU.add), r=[("bb", 1), "tb"], w=[("bb", 1)])
        S.dve(lambda e: e.tensor_copy(out=Y[0:64, 0, :, :], in_=bb[0:64, 0, :, :]), r=[("bb", 0)], w=[("Y", 0)])
        S.dve(lambda e: e.tensor_scalar(out=Y[0:64, 1, :, :], in0=bb[0:64, 1, :, :], scalar1=-1.0, scalar2=None, op0=ALU.mult), r=[("bb", 1)], w=[("Y", 1)])
        S.dve(lambda e: e.tensor_copy(out=Y[64:128, 0, :, :], in_=bb[64:128, 1, :, :]), r=[("bb", 1)], w=[("Y", 2)])
        S.dve(lambda e: e.tensor_copy(out=Y[64:128, 1, :, :], in_=bb[64:128, 0, :, :]), r=[("bb", 0)], w=[("Y", 3)])
        Ytok = [("Y", i) for i in range(4)]
        if PREP_STAGE <= 2:
            S.barrier()
            return
        crow = k.sb(es1, [128, 2, 8, 2, 64], F32, "crow")
        cT = k.sb(es1, [128, 2, 64, 16], F32, "cT")
        for ri, nm in enumerate(("c_re", "c_im")):
            for dup in range(2):
                S.dma("sp", crow[:, ri, :, dup, :], p[nm].rearrange("d g h p -> (d g h) p").rearrange("(t r) p -> r t p", r=128),
                      w=[("crow", ri, dup)])
            for t in range(8):
                b = k.bank()
                S.pe(lambda e, ri=ri, t=t, b=b: e.transpose(k.ps[b][:, 0:128], crow[:, ri, t, :, :].rearrange("p a b -> p (a b)"), k.ident[:, :]),
                     r=[("crow", ri, 0), ("crow", ri, 1), "ident"], w=[("ps", b)])
                S.act(lambda e, ri=ri, t=t, b=b: e.activation(out=cT[:, ri, t * 8:(t + 1) * 8, :], in_=k.ps[b][:, 0:128], func=AF.Copy),
                      r=[("ps", b)], w=[("cT", ri)])
        S.dve(lambda e: e.tensor_copy(out=X[0:64, 0, :, :], in_=cT[0:64, 0, :, :]), r=[("cT", 0)], w=[("X", 0)])
        S.dve(lambda e: e.tensor_scalar(out=X[0:64, 1, :, :], in0=cT[0:64, 1, :, :], scalar1=-1.0, scalar2=None, op0=ALU.mult), r=[("cT", 1)], w=[("X", 1)])
        S.dve(lambda e: e.tensor_scalar(out=X[64:128, 0, :, :], in0=cT[64:128, 1, :, :], scalar1=-1.0, scalar2=None, op0=ALU.mult), r=[("cT", 1)], w=[("X", 2)])
        S.dve(lambda e: e.tensor_scalar(out=X[64:128, 1, :, :], in0=cT[64:128, 0, :, :], scalar1=-1.0, scalar2=None, op0=ALU.mult), r=[("cT", 0)], w=[("X", 3)])
        Xtok = [("X", i) for i in range(4)]
        S.barrier()
        es1.close()
        if PREP_STAGE <= 3:
            S.barrier()
            return
        S.pool(lambda e: e.memset(T["Wfar"][:, :, :, :], 0.0), w=["Wfar"])
        CF = k.sb(es, [128, 9, 32, 16], F32, "CF")
        BL = k.sb(es, [128, 9, 32, 16], F32, "BL")
        tq = k.sb(es, [128, 9, 32, 16], F32, "tq")
        for d in range(2):
            gs = slice(d * 32, (d + 1) * 32)
            lre4 = bc(lre[:, gs, :].rearrange("p g k -> p k g").unsqueeze(3), [128, 9, 32, 16])
            lim4 = bc(lim[:, gs, :].rearrange("p g k -> p k g").unsqueeze(3), [128, 9, 32, 16])
            for (dst, src, stok, dtok) in ((CF, X, Xtok, "CF"), (BL, Y, Ytok, "BL")):
                s1 = bc(src[:, 0, gs, :].unsqueeze(1), [128, 9, 32, 16])
                s2 = bc(src[:, 1, gs, :].unsqueeze(1), [128, 9, 32, 16])
                S.dve(lambda e, dst=dst, s1=s1, lre4=lre4: e.tensor_tensor(out=dst[:, :, :, :], in0=s1, in1=lre4, op=ALU.mult),
                      r=stok + ["lre"], w=[dtok])
                S.pool(lambda e, s2=s2, lim4=lim4: e.tensor_tensor(out=tq[:, :, :, :], in0=s2, in1=lim4, op=ALU.mult),
                       r=stok + ["lim"], w=["tq"])
                S.dve(lambda e, dst=dst: e.tensor_tensor(out=dst[:, :, :, :], in0=dst[:, :, :, :], in1=tq[:, :, :, :], op=ALU.add),
                      r=[dtok, "tq"], w=[dtok])
            for par in range(2 if PREP_STAGE > 4 else 0):
                for j in range(TAU):
                    kk_ = j + 1 if d == 0 else TAU - j
                    S.act(lambda e, par=par, j=j, kk_=kk_, d=d: e.activation(
                        out=T["Wfar"][:, d * 32 + par:(d + 1) * 32:2, j, 16 * par:16 * par + 16],
                        in_=CF[:, kk_, par:32:2, :], func=AF.Copy), r=["CF", "Wfar"], w=["Wfar"])
            for q in range(4 if PREP_STAGE > 5 else 0):
                for lag in range(TAU):
                    b = k.bank()
                    S.pe(lambda e, b=b, q=q, lag=lag: e.matmul(k.ps[b][:, 0:128], BL[:, lag, q * 8:(q + 1) * 8, :].rearrange("p g h -> p (g h)"),
                                                               CF[:, 0, q * 8:(q + 1) * 8, :].rearrange("p g h -> p (g h)"), start=True, stop=True),
                         r=["BL", "CF"], w=[("ps", b)])
                    S.dve(lambda e, b=b, q=q, lag=lag, d=d: e.tensor_tensor(out=T["Knear"][:, q, d * 8 + lag, :], in0=k.ps[b][:, 0:128],
                                                                            in1=T["bdmask"][:, :], op=ALU.mult),
                          r=[("ps", b), "bdmask"], w=["Knear"])
                    b2 = k.bank()
                    S.pe(lambda e, b2=b2, q=q, lag=lag: e.transpose(k.ps[b2][:, 0:128], BL[:, lag, q * 8:(q + 1) * 8, :].rearrange("p g h -> p (g h)"), k.ident[:, :]),
                         r=["BL", "ident"], w=[("ps", b2)])
                    i_ = (TAU - 1 - lag) if d == 0 else lag
                    for par in range(2 if PREP_STAGE > 6 else 0):
                        S.act(lambda e, b2=b2, q=q, i_=i_, par=par, d=d: e.activation(
                            out=T["Wup"][:, par, d * 4 + q, i_, :], in_=k.ps[b2][:, 0:128], func=AF.Identity,
                            bias=0.0, scale=T["pmask"][:, par:par + 1]), r=[("ps", b2), "pmask"], w=["Wup"])
        S.barrier()


def rev_axis(ap, axis):
    pat = [list(x) for x in ap.ap]
    st, n = pat[axis]
    off = ap.offset + st * (n - 1)
    pat[axis] = [-st, n]
    return bass.AP(ap.tensor, off, pat)


def phase_s5a(k, T, UT_d, SIN_d):
    S = k.S
    CB = 32
    NB = NCH // CB
    with ExitStack() as es:
        ublk = [[k.sb(es, [128, 4, CB * TAU], BF16, "ublk") for _ in range(2)] for _ in range(2)]
        Zb = [k.sb(es, [64, CB, 2, 2, 32, 2], F32, "Zb") for _ in range(2)]
        carry = k.sb(es, [64, 2, 2, 32, 2], F32, "carry")
        m1 = k.sb(es, [64, 2, 2, 32, 2], F32, "m1")
        m2 = k.sb(es, [64, 2, 2, 32, 2], F32, "m2")
        Sb = [[k.sb(es, [128, 2, CB, 32], BF16, "Sb") for _ in range(2)] for _ in range(2)]
        zs = k.sb(es, [128, 32], BF16, "zs")
        S.dve(lambda e: e.memset(zs[:, :], 0.0), w=["zs"])
        S.dve(lambda e: e.memset(carry[:, :, :, :, :], 0.0), w=["carry"])
        for s in range(NSEQ):
            S.dma("pool", SIN_d[s][0][:, 0, :], zs[:, :], r=["zs"], stream="st")
            S.dma("pool", SIN_d[s][1][:, CTX // TAU - 1, :], zs[:, :], r=["zs"], stream="st")
        order = {0: list(range(NB)), 1: [0] + list(range(NB - 1, 0, -1))}

        def stage1(step):
            par = step % 2
            Z = Zb[par]
            ztok = ("Zb", par)
            for d in range(2):
                B = order[d][step]
                for s in range(NSEQ):
                    S.dma("sp", ublk[d][s][:, :, :], UT_d[s][:, :, B * CB * TAU:(B + 1) * CB * TAU], w=[("ublk", d, s)])
                for s in range(NSEQ):
                    bq = [k.bank() for _ in range(4)]
                    for slot in range(8):
                        q, pr = slot // 2, slot % 2
                        for i in range(TAU):
                            for qd in range(4):
                                S.pe(lambda e, bk=bq[qd], slot=slot, q=q, qd=qd, pr=pr, i=i, d=d, s=s: e.matmul(
                                    k.ps[bk][:, slot * CB:(slot + 1) * CB], T["Wup"][32 * qd:32 * qd + 32, pr, d * 4 + q, i, :],
                                    ublk[d][s][32 * qd:32 * qd + 32, q, i:CB * TAU:TAU], start=(i == 0), stop=(i == TAU - 1),
                                    tile_position=(32 * qd, 0), skip_group_check=True),
                                    r=["Wup", ("ublk", d, s)], w=[("ps", bq[qd])])
                    for qd in range(4):
                        for ri in range(2):
                            src = k.ps[bq[qd]][ri * 64:(ri + 1) * 64, 0:8 * CB].rearrange("p (q r c) -> p q r c", q=4, r=2)
                            if d == 1:
                                src = rev_axis(src, 3)
                            S.act(lambda e, qd=qd, ri=ri, s=s, Z=Z, d=d, src=src: e.activation(
                                out=Z[:, :, ri, d, :, s].rearrange("p c (q m) -> p q m c", m=8)[:, :, 2 * qd:2 * qd + 2, :],
                                in_=src, func=AF.Copy), r=[("ps", bq[qd])], w=[(ztok, d)])

        stage1(0)
        for step in range(NB):
            par = step % 2
            Z = Zb[par]
            ztok = ("Zb", par)
            if step + 1 < NB:
                stage1(step + 1)
            zr = [(ztok, 0), (ztok, 1)]
            prev = carry[:, :, :, :, :]
            ptok = ["carry"]
            for kk in range(CB):
                cur = Z[:, kk, :, :, :, :]
                S.dve(lambda e, prev=prev: e.tensor_tensor(out=m1[:, :, :, :, :], in0=prev, in1=T["LRm"][:, :, :, :, :], op=ALU.mult),
                      r=ptok + ["LRm"], w=["m1"])
                S.dve(lambda e, prev=prev: e.tensor_tensor(out=m2[:, :, :, :, :], in0=rev_axis(prev, 1), in1=T["LIm"][:, :, :, :, :], op=ALU.mult),
                      r=ptok + ["LIm"], w=["m2"])
                S.dve(lambda e, cur=cur: e.tensor_tensor(out=m1[:, :, :, :, :], in0=m1[:, :, :, :, :], in1=cur, op=ALU.add),
                      r=["m1"] + zr, w=["m1"])
                S.dve(lambda e, cur=cur: e.tensor_tensor(out=cur, in0=m1[:, :, :, :, :], in1=m2[:, :, :, :, :], op=ALU.add),
                      r=["m1", "m2"], w=zr)
                prev = cur
                ptok = zr
            S.dve(lambda e, prev=prev: e.tensor_copy(out=carry[:, :, :, :, :], in_=prev), r=zr, w=["carry"])
            for d in range(2):
                B = order[d][step]
                sb_ = Sb[d][par]
                stok = ("Sb", d, par)
                for ri in range(2):
                    src = Z[:, :, ri, d, :, :].rearrange("p c g s -> p s c g")
                    if d == 1:
                        src = rev_axis(src, 2)
                    S.act(lambda e, ri=ri, sb_=sb_, src=src: e.activation(out=sb_[ri * 64:(ri + 1) * 64, :, :, :], in_=src, func=AF.Copy),
                          r=zr, w=[stok])
                c0 = B * CB
                for s in range(NSEQ):
                    if d == 0:
                        n = CB if B < NB - 1 else CB - 1
                        S.dma("pool", SIN_d[s][0][:, c0 + 1:c0 + 1 + n, :], sb_[:, s, 0:n, :], r=[stok], stream="st")
                    else:
                        lo = 1 if B <= 1 else 0
                        S.dma("pool", SIN_d[s][1][:, c0 + lo - 1:c0 + CB - 1, :], sb_[:, s, lo:CB, :], r=[stok], stream="st")
                        if B == 0:
                            S.dma("pool", SIN_d[s][1][:, NCH - 1, :], sb_[:, s, 0, :], r=[stok], stream="st")
        S.barrier()


def phase_s5b(k, T, UT_d, SIN_d, ST_d, p):
    S = k.S
    with ExitStack() as es:
        wglu = k.sb(es, [128, 4, 512], BF16, "wglu")
        cols = load_cols(k, es, [p["ssm_d"], p["b_glu"]], "s5cols")
        ublk = [k.sb(es, [128, 4, 512], BF16, "ublk") for _ in range(2)]
        sin = [[k.sb(es, [128, 64, 32], BF16, "sin") for _ in range(2)] for _ in range(2)]
        yf = [k.sb(es, [128, 512], F32, "yf") for _ in range(2)]
        yg = [k.sb(es, [128, 4, 512], BF16, "yg") for _ in range(2)]
        sgl = [k.sb(es, [128, 512], F32, "sgl") for _ in range(2)]
        so = [k.sb(es, [128, 4, 512], BF16, "so") for _ in range(2)]
        for q in range(4):
            S.dma("pool", wglu[:, q, :], p["w_glu"][q * 128:(q + 1) * 128, :], w=[("wglu", q)], stream="ldw")
        bi = 0
        ny = 0
        for s in range(NSEQ):
            for (t0, Tn, isctx) in seq_blocks(512):
                NC = Tn // TAU
                c0 = t0 // TAU
                par = bi % 2
                u = ublk[par]
                utok = ("ublk", par)
                S.dma("sp", u[:, :, 0:Tn], UT_d[s][:, :, t0:t0 + Tn], w=[utok])
                for d in range(2):
                    S.dma("sp", sin[par][d][:, 0:NC, :], SIN_d[s][d][:, c0:c0 + NC, :], w=[("sin", par, d)])
                ygt = yg[par]
                for q in range(4):
                    bk = k.bank()
                    for j in range(TAU):
                        out = k.ps[bk][:, j:Tn:TAU]
                        for i in range(TAU):
                            slots = []
                            if i <= j:
                                slots.append(j - i)
                            if i >= j:
                                slots.append(8 + i - j)
                            for sl in slots:
                                first = (i == 0 and sl == slots[0])
                                S.pe(lambda e, out=out, q=q, sl=sl, i=i, u=u, Tn=Tn, first=first: e.matmul(
                                    out, T["Knear"][:, q, sl, :], u[:, q, i:Tn:TAU], start=first, stop=False, skip_group_check=True),
                                    r=["Knear", utok], w=[("ps", bk)])
                        for qd in range(4):
                            for pr in range(2):
                                g = q * 8 + qd * 2 + pr
                                for d in range(2):
                                    last = (pr == 1 and d == 1)
                                    S.pe(lambda e, bk=bk, j=j, qd=qd, g=g, d=d, Tn=Tn, NC=NC, last=last, par=par: e.matmul(
                                        k.ps[bk][32 * qd:32 * qd + 32, j:Tn:TAU], T["Wfar"][:, d * 32 + g, j, :],
                                        sin[par][d][:, 0:NC, g], start=False, stop=last, skip_group_check=True,
                                        tile_position=(0, 32 * qd)),
                                        r=["Wfar", ("sin", par, d)], w=[("ps", bk)])
                    y_ = yf[ny % 2]
                    S.dve(lambda e, bk=bk, q=q, u=u, Tn=Tn, y_=y_: e.scalar_tensor_tensor(
                        out=y_[:, 0:Tn], in0=u[:, q, 0:Tn], scalar=cols[:, 0, q:q + 1], in1=k.ps[bk][:, 0:Tn],
                        op0=ALU.mult, op1=ALU.add), r=[("ps", bk), utok, "s5cols"], w=[("yf", ny % 2)])
                    S.act(lambda e, q=q, Tn=Tn, y_=y_, ygt=ygt: e.activation(out=ygt[:, q, 0:Tn], in_=y_[:, 0:Tn], func=AF.Gelu_apprx_tanh),
                          r=[("yf", ny % 2)], w=[("yg", par, q)])
                    ny += 1
                so_ = so[par]
                for m in range(4):
                    bk = k.bank()
                    for q in range(4):
                        S.pe(lambda e, bk=bk, q=q, m=m, Tn=Tn, ygt=ygt: e.matmul(
                            k.ps[bk][:, 0:Tn], wglu[:, q, m * 128:(m + 1) * 128], ygt[:, q, 0:Tn], start=(q == 0), stop=(q == 3)),
                            r=[("wglu", q), ("yg", par, q)], w=[("ps", bk)])
                    sg_ = sgl[m % 2]
                    S.act(lambda e, bk=bk, m=m, Tn=Tn, sg_=sg_: e.activation(out=sg_[:, 0:Tn], in_=k.ps[bk][:, 0:Tn], func=AF.Sigmoid,
                                                                           bias=cols[:, 1, m:m + 1], scale=1.0),
                          r=[("ps", bk), "s5cols"], w=[("sgl", m % 2)])
                    S.dve(lambda e, m=m, Tn=Tn, sg_=sg_, ygt=ygt, so_=so_: e.tensor_tensor(
                        out=so_[:, m, 0:Tn], in0=ygt[:, m, 0:Tn], in1=sg_[:, 0:Tn], op=ALU.mult),
                        r=[("yg", par, m), ("sgl", m % 2)], w=[("so", par, m)])
                S.dma("pool", ST_d[s][:, :, t0:t0 + Tn], so_[:, :, 0:Tn], r=[("so", par, m) for m in range(4)], stream="st")
                bi += 1
        S.barrier()


HD = 128
NH = 8
NKV = 2
SM_SCALE = HD ** -0.5
DEN_DVE_EVERY = 0


def qk_norm_rope(k, ps_src, nh, gbc, gtok, rc, rs, tile_i, dst, dtok, ss, sstok, tq, tqtok, rope):
    S = k.S
    for h in range(nh):
        src, srctok = ps_src[h]
        S.act(lambda e, src=src, h=h: e.activation(out=k.junk[:, h, :], in_=src, func=AF.Square, accum_out=ss[:, h:h + 1]),
              r=[srctok], w=[(sstok, h), ("junk", h)])
    S.act(lambda e: e.activation(out=ss[:, 8:8 + nh], in_=ss[:, 0:nh], func=AF.Ln, bias=RMS_EPS, scale=1.0 / HD),
          r=[(sstok, h) for h in range(nh)], w=[(sstok, "m")])
    S.act(lambda e: e.activation(out=ss[:, 16:16 + nh], in_=ss[:, 8:8 + nh], func=AF.Exp, scale=-0.5),
          r=[(sstok, "m")], w=[(sstok, "r")])
    tgt = tq if rope else dst
    wt = [tqtok, (tqtok, 1), (tqtok, 2)] if rope else [(dtok, 0), (dtok, 1)]
    for h in range(nh):
        src, srctok = ps_src[h]
        S.dve(lambda e, src=src, h=h: e.scalar_tensor_tensor(out=tgt[:, h * 128:(h + 1) * 128], in0=src, scalar=ss[:, 16 + h:17 + h],
                                                             in1=gbc[:, :], op0=ALU.mult, op1=ALU.mult),
              r=[srctok, (sstok, "r"), gtok], w=wt)
    if not rope:
        return
    x3 = tq[:, 0:nh * 128].rearrange("p (h d) -> p h d", d=128)
    o3 = dst[:, 0:nh * 128].rearrange("p (h d) -> p h d", d=128)
    c3 = bc(rc[:, tile_i, :].unsqueeze(1), [128, nh, 64])
    s3 = bc(rs[:, tile_i, :].unsqueeze(1), [128, nh, 64])
    x1, x2 = x3[:, :, 0:64], x3[:, :, 64:128]
    S.dve(lambda e: e.tensor_tensor(out=o3[:, :, 0:64], in0=x1, in1=c3, op=ALU.mult), r=[tqtok, "rope"], w=[(dtok, 0)])
    S.dve(lambda e: e.tensor_tensor(out=o3[:, :, 64:128], in0=x2, in1=c3, op=ALU.mult), r=[tqtok, "rope"], w=[(dtok, 1)])
    S.dve(lambda e: e.tensor_tensor(out=x2, in0=x2, in1=s3, op=ALU.mult), r=[tqtok, (dtok, 1), "rope"], w=[(tqtok, 2)])
    S.dve(lambda e: e.tensor_tensor(out=x1, in0=x1, in1=s3, op=ALU.mult), r=[tqtok, (dtok, 0), "rope"], w=[(tqtok, 1)])
    S.dve(lambda e: e.tensor_tensor(out=o3[:, :, 0:64], in0=o3[:, :, 0:64], in1=x2, op=ALU.subtract), r=[(dtok, 0), (tqtok, 2)], w=[(dtok, 0)])
    S.dve(lambda e: e.tensor_tensor(out=o3[:, :, 64:128], in0=o3[:, :, 64:128], in1=x1, op=ALU.add), r=[(dtok, 1), (tqtok, 1)], w=[(dtok, 1)])


def phase_l1a(k, X2_d, wc_d, kg_d, KT_d, V_d, rope_d):
    S = k.S
    with ExitStack() as es:
        wkv = k.sb(es, [128, 8, 512], BF16, "wkv")
        kg = k.sb(es, [128, HD], F32, "kg")
        rc = k.sb(es, [128, 32, 64], F32, "rc")
        rs = k.sb(es, [128, 32, 64], F32, "rs")
        xin = [k.sb(es, [128, 4, D], F32, "xin") for _ in range(2)]
        hT = [k.sb(es, [128, 8, 512], BF16, "hT") for _ in range(2)]
        kf = [k.sb(es, [128, 256], F32, "kf") for _ in range(8)]
        tq = [k.sb(es, [128, 256], F32, "tq") for _ in range(8)]
        ss = [k.sb(es, [128, 24], F32, "ss") for _ in range(8)]
        vb = [k.sb(es, [128, 256], BF16, "vb") for _ in range(8)]
        kTs = [k.sb(es, [128, 2, 512], BF16, "kTs") for _ in range(2)]
        for kc in range(8):
            S.dma("pool", wkv[:, kc, :], wc_d[kc * 128:(kc + 1) * 128, 1024:1536], w=[("wkv", kc)], stream="ldw")
        load_rows_bc(k, kg, kg_d, "kg")
        S.dma("sp", rc[:, :, :], rope_d[0].rearrange("(t p) i -> p t i", p=128), w=["rope"])
        S.dma("sp", rs[:, :, :], rope_d[1].rearrange("(t p) i -> p t i", p=128), w=["rope"])
        blks = [(b, t0, T, isctx) for b in range(NSEQ) for (t0, T, isctx) in seq_blocks(512)]

        def stage_a(bi):
            b, t0, T, isctx = blks[bi]
            row = 2 if isctx else b
            ntile = T // 128
            x, xt = xin[bi % 2], ("xin", bi % 2)
            h, ht = hT[bi % 2], ("hT", bi % 2)
            for tt in range(ntile):
                S.dma("sp", x[:, tt, :], X2_d[b][t0 + tt * 128:t0 + (tt + 1) * 128, :], w=[(xt, tt)])
            mod_transpose(k, x, xt, ntile, h, ht, 1, 8, 0, row)
            for tt in range(ntile):
                u = (bi % 2) * 4 + tt
                bk = k.bank()
                for kc in range(8):
                    S.pe(lambda e, bk=bk, kc=kc, tt=tt, h=h: e.matmul(k.ps[bk][:, :], h[:, kc, tt * 128:(tt + 1) * 128], wkv[:, kc, :],
                                                                      start=(kc == 0), stop=(kc == 7)),
                         r=[("wkv", kc), (ht, kc)], w=[("ps", bk)])
                v_ = vb[u]
                S.act(lambda e, bk=bk, v_=v_: e.activation(out=v_[:, :], in_=k.ps[bk][:, 256:512], func=AF.Copy),
                      r=[("ps", bk)], w=[("vb", u)])
                S.dma("pool", V_d[b][t0 + tt * 128:t0 + (tt + 1) * 128, :], v_[:, :], r=[("vb", u)], stream="st")
                srcs = [(k.ps[bk][:, hh * 128:(hh + 1) * 128], ("ps", bk)) for hh in range(2)]
                ti = (t0 - CTX) // 128 + tt if not isctx else 0
                qk_norm_rope(k, srcs, 2, kg, "kg", rc, rs, ti, kf[u], ("kf", u), ss[u], ("ss", u),
                             tq[u], ("tq", u), rope=not isctx)

        def stage_b(bi):
            b, t0, T, isctx = blks[bi]
            ntile = T // 128
            kt_ = kTs[bi % 2]
            for tt in range(ntile):
                u = (bi % 2) * 4 + tt
                kft = [(("kf", u), 0), (("kf", u), 1)]
                for hh in range(2):
                    b2 = k.bank()
                    S.pe(lambda e, b2=b2, hh=hh, u=u: e.transpose(k.ps[b2][:, 0:128], kf[u][:, hh * 128:(hh + 1) * 128], k.ident[:, :]),
                         r=kft + ["ident"], w=[("ps", b2)])
                    S.act(lambda e, b2=b2, hh=hh, tt=tt, kt_=kt_: e.activation(out=kt_[:, hh, tt * 128:(tt + 1) * 128], in_=k.ps[b2][:, 0:128], func=AF.Copy),
                          r=[("ps", b2)], w=[("kTs", bi % 2, tt)])
            S.dma("pool", KT_d[b][:, :, t0:t0 + T], kt_[:, :, 0:T], r=[("kTs", bi % 2, tt) for tt in range(ntile)], stream="st")

        stage_a(0)
        for bi in range(len(blks)):
            if bi + 1 < len(blks):
                stage_a(bi + 1)
            stage_b(bi)
        S.barrier()


def phase_l1b(k, X2_d, wc_d, qg_d, wo_d, KT_d, V_d, rope_d, lng_d, lnb_d, grow_d, X3_d):
    S = k.S
    NKT = NTOK // 128
    with ExitStack() as es:
        wq = k.sb(es, [128, 8, D], BF16, "wq")
        wo = k.sb(es, [128, 8, D], BF16, "wo")
        qg = k.sb(es, [128, HD], F32, "qg")
        rc = k.sb(es, [128, 32, 64], F32, "rc")
        rs = k.sb(es, [128, 32, 64], F32, "rs")
        lng = k.sb(es, [128, D], F32, "lng")
        lnb = k.sb(es, [128, D], F32, "lnb")
        gate = k.sb(es, [128, D], F32, "gate")
        KT = k.sb(es, [128, 2, NTOK], BF16, "KT")
        V = k.sb(es, [128, NKT, 256], BF16, "V")
        xin = [k.sb(es, [128, 4, D], F32, "xin") for _ in range(2)]
        hT = k.sb(es, [128, 8, 512], BF16, "hT")
        qf = [k.sb(es, [128, D], F32, "qf") for _ in range(4)]
        tq = k.sb(es, [128, D], F32, "tq")
        ss = [k.sb(es, [128, 24], F32, "ss") for _ in range(2)]
        QT = [k.sb(es, [128, 8, 512], BF16, "QT") for _ in range(2)]
        pT = [k.sb(es, [128, 512], BF16, "pT") for _ in range(4)]
        rden = k.sb(es, [128, 512], F32, "rden")
        dacc = k.sb(es, [128, 512], F32, "dacc")
        OT = [k.sb(es, [128, 8, 512], BF16, "OT") for _ in range(2)]
        tmp = k.sb(es, [128, D], F32, "tmp")
        st = [k.sb(es, [128, 20], F32, "st") for _ in range(2)]
        for kc in range(8):
            S.dma("pool", wq[:, kc, :], wc_d[kc * 128:(kc + 1) * 128, 0:1024], w=[("wq", kc)], stream="ldw")
            S.dma("pool", wo[:, kc, :], wo_d[kc * 128:(kc + 1) * 128, :], w=[("wo", kc)], stream="ldw")
        load_rows_bc(k, qg, qg_d, "qg")
        load_rows_bc(k, lng, lng_d, "lng")
        load_rows_bc(k, lnb, lnb_d, "lnb")
        S.dma("sp", rc[:, :, :], rope_d[0].rearrange("(t p) i -> p t i", p=128), w=["rope"])
        S.dma("sp", rs[:, :, :], rope_d[1].rearrange("(t p) i -> p t i", p=128), w=["rope"])
        blks = [(b, qb) for b in range(NSEQ) for qb in range(SEQ // 512)]
        cnt = {"pt": 0, "qk": 0, "gate": None, "kv": None}

        def load_x(i):
            b, qb = blks[i]
            t0 = CTX + qb * 512
            x, xt = xin[i % 2], ("xin", i % 2)
            for tt in range(4):
                S.dma("sp", x[:, tt, :], X2_d[b][t0 + tt * 128:t0 + (tt + 1) * 128, :], w=[(xt, tt)])

        def prep_a(i):
            b, qb = blks[i]
            x, xt = xin[i % 2], ("xin", i % 2)
            mod_transpose(k, x, xt, 4, hT, "hT", 1, 8, 0, b)
            for tt in range(4):
                qb_ = (k.bank(), k.bank())
                for hh in range(2):
                    for kc in range(8):
                        S.pe(lambda e, hh=hh, kc=kc, tt=tt, qb_=qb_: e.matmul(
                            k.ps[qb_[hh]][:, :], hT[:, kc, tt * 128:(tt + 1) * 128], wq[:, kc, hh * 512:(hh + 1) * 512],
                            start=(kc == 0), stop=(kc == 7)), r=[("wq", kc), ("hT", kc)], w=[("ps", qb_[hh])])
                for hh in range(2):
                    S.act(lambda e, hh=hh, tt=tt, qb_=qb_: e.activation(out=qf[tt][:, hh * 512:(hh + 1) * 512], in_=k.ps[qb_[hh]][:, :], func=AF.Copy),
                          r=[("ps", qb_[hh])], w=[("qfraw", tt, hh), (("qf", tt), 0), (("qf", tt), 1)])
                srcs = [(qf[tt][:, h * 128:(h + 1) * 128], ("qfraw", tt, h // 4)) for h in range(8)]
                qk_norm_rope(k, srcs, 8, qg, "qg", rc, rs, qb * 4 + tt, qf[tt], ("qf", tt), ss[tt % 2], ("ss", tt % 2),
                             tq, "tq", rope=True)

        def prep_b(i):
            Q = QT[i % 2]
            for tt in range(4):
                for h in range(8):
                    b2 = k.bank()
                    S.pe(lambda e, b2=b2, h=h, tt=tt: e.transpose(k.ps[b2][:, 0:128], qf[tt][:, h * 128:(h + 1) * 128], k.ident[:, :]),
                         r=[(("qf", tt), 0), (("qf", tt), 1), "ident"], w=[("ps", b2)])
                    if h % 2 == 0:
                        S.act(lambda e, b2=b2, h=h, tt=tt, Q=Q: e.activation(out=Q[:, h, tt * 128:(tt + 1) * 128], in_=k.ps[b2][:, 0:128], func=AF.Copy),
                              r=[("ps", b2)], w=[("QT", i % 2, h)])
                    else:
                        S.dve(lambda e, b2=b2, h=h, tt=tt, Q=Q: e.tensor_copy(out=Q[:, h, tt * 128:(tt + 1) * 128], in_=k.ps[b2][:, 0:128]),
                              r=[("ps", b2)], w=[("QT", i % 2, h)])

        def head(i, h):
            b, qb = blks[i]
            if cnt["kv"] != b:
                cnt["kv"] = b
                S.dma("sp", KT[:, :, :], KT_d[b], w=["KT"])
                S.dma("sp", V[:, :, :], V_d[b].rearrange("(t p) c -> p t c", p=128), w=["V"])
            Q = QT[i % 2]
            O = OT[i % 2]
            kvh = h // 4
            bo, bd = (0, 1) if h % 2 == 0 else (2, 3)
            pend = []

            def issue_pv(item):
                kt, p_, ptok = item
                S.pe(lambda e: e.matmul(k.ps[bo][:, :], V[:, kt, kvh * 128:(kvh + 1) * 128], p_[:, :],
                                        start=(kt == 0), stop=(kt == NKT - 1)), r=["V", ptok], w=[("ps", bo)])
                if DEN_DVE_EVERY and kt % DEN_DVE_EVERY == 1:
                    if kt == 1:
                        S.dve(lambda e: e.tensor_copy(out=dacc[:, :], in_=p_[:, :]), r=[ptok], w=["dacc"])
                    else:
                        S.dve(lambda e: e.tensor_tensor(out=dacc[:, :], in0=dacc[:, :], in1=p_[:, :], op=ALU.add), r=[ptok, "dacc"], w=["dacc"])
                else:
                    S.pe(lambda e: e.matmul(k.ps[bd][:, :], k.ones_b[:, :], p_[:, :], start=(kt == 0), stop=False),
                         r=["ones_b", ptok], w=[("ps", bd)])
            for kt in range(NKT):
                bs = 4 + cnt["qk"] % 4
                cnt["qk"] += 1
                S.pe(lambda e, bs=bs, kt=kt: e.matmul(k.ps[bs][:, :], KT[:, kvh, kt * 128:(kt + 1) * 128], Q[:, h, :],
                                                      start=True, stop=True), r=["KT", ("QT", i % 2, h)], w=[("ps", bs)])
                p_ = pT[cnt["pt"] % 4]
                ptok = ("pT", cnt["pt"] % 4)
                cnt["pt"] += 1
                S.act(lambda e, bs=bs, p_=p_: e.activation(out=p_[:, :], in_=k.ps[bs][:, :], func=AF.Exp, scale=SM_SCALE),
                      r=[("ps", bs)], w=[ptok])
                pend.append((kt, p_, ptok))
                if len(pend) > 2:
                    issue_pv(pend.pop(0))
            while pend:
                issue_pv(pend.pop(0))
            if DEN_DVE_EVERY:
                S.pe(lambda e: e.matmul(k.ps[bd][:, :], k.ones_f[:, :], dacc[:, :], start=False, stop=True),
                     r=["ones_f", "dacc"], w=[("ps", bd)])
            S.dve(lambda e: e.reciprocal(out=rden[:, :], in_=k.ps[bd][:, :]), r=[("ps", bd)], w=["rden"])
            S.dve(lambda e: e.tensor_tensor(out=O[:, h, :], in0=k.ps[bo][:, :], in1=rden[:, :], op=ALU.mult),
                  r=[("ps", bo), "rden"], w=[("OT", i % 2, h)])

        def post(i):
            b, qb = blks[i]
            if cnt["gate"] != b:
                cnt["gate"] = b
                load_rows_bc(k, gate, grow_d[1, 0, b], "gate")
            x, xt = xin[i % 2], ("xin", i % 2)
            O = OT[i % 2]
            for tt in range(4):
                yb = (k.bank(), k.bank())
                for hh in range(2):
                    for h in range(8):
                        S.pe(lambda e, hh=hh, h=h, tt=tt, yb=yb: e.matmul(
                            k.ps[yb[hh]][:, :], O[:, h, tt * 128:(tt + 1) * 128], wo[:, h, hh * 512:(hh + 1) * 512],
                            start=(h == 0), stop=(h == 7)), r=[("OT", i % 2, h), ("wo", h)], w=[("ps", yb[hh])])
                ln_epilogue(k, yb, x[:, tt, :], (xt, tt), gate, "gate", lng, lnb, tmp, "tmp",
                            st[tt % 2], ("st", tt % 2), x[:, tt, :], (xt, tt), gb="dve")
                S.dma("pool", X3_d[b][qb * 512 + tt * 128:qb * 512 + (tt + 1) * 128, :], x[:, tt, :], r=[(xt, tt)], stream="st")

        n = len(blks)
        load_x(0)
        prep_a(0)
        prep_b(0)
        load_x(1)
        for i in range(n):
            head(i, 0)
            head(i, 1)
            if i > 0:
                post(i - 1)
                if i + 1 < n:
                    load_x(i + 1)
            head(i, 2)
            head(i, 3)
            if i + 1 < n:
                prep_a(i + 1)
            head(i, 4)
            head(i, 5)
            head(i, 6)
            if i + 1 < n:
                prep_b(i + 1)
            head(i, 7)
        post(n - 1)
        S.barrier()


IN_SPECS = [
    ("x", [NSEQ, SEQ, D]), ("c", [NSEQ, D]), ("ctx", [NSEQ, CTX, D]), ("c_ctx", [D]),
    ("w_mod", [2, D, 6 * D]), ("b_mod", [2, 6 * D]), ("ln_g", [2, 2, D]), ("ln_b", [2, 2, D]),
    ("w_in_ab", [1, D, 1536]), ("conv_w", [1, 31, 512]), ("conv_b", [1, 512]), ("conv_ln_g", [1, 512]),
    ("conv_ln_b", [1, 512]), ("ssm_lambda_re", [1, 2, 32, 64]), ("ssm_lambda_im", [1, 2, 32, 64]),
    ("ssm_log_dt", [1, 2, 32]), ("ssm_b_re", [1, 2, 32, 64, 16]), ("ssm_b_im", [1, 2, 32, 64, 16]),
    ("ssm_c_re", [1, 2, 32, 16, 64]), ("ssm_c_im", [1, 2, 32, 16, 64]), ("ssm_d", [1, 512]),
    ("ssm_w_glu", [1, 512, 512]), ("ssm_b_glu", [1, 512]), ("w_out_ab", [1, D, D]), ("w_in_c", [1, D, 1536]),
    ("q_norm_g", [1, HD]), ("k_norm_g", [1, HD]), ("w_out_c", [1, D, D]), ("w_ffn_in", [2, D, 2 * DFF]),
    ("w_ffn_out", [2, DFF, D]),
    ("cst_bdmask", [128, 128]), ("cst_pmask", [128, 2]), ("cst_rope", [2, SEQ, 64]),
]


def host_consts():
    r = np.arange(128)
    bd = (r[:, None] // 16 == r[None, :] // 16).astype(np.float32)
    pm = np.stack([((r // 16) % 2 == 0), ((r // 16) % 2 == 1)], 1).astype(np.float32)
    pos = np.arange(SEQ)
    freqs = (10000.0 ** (-np.arange(0, 64, 2, dtype=np.float64) / 64.0))
    ang = np.concatenate([(pos // 64)[:, None] * freqs[None, :], (pos % 64)[:, None] * freqs[None, :]], 1)
    rope = np.stack([np.cos(ang), np.sin(ang)], 0).astype(np.float32)
    return {"cst_bdmask": bd, "cst_pmask": pm, "cst_rope": rope}


def s5_tables(k, es, d):
    T = {
        "Wup": k.sb(es, [128, 2, 8, TAU, 128], BF16, "Wup"),
        "Wfar": k.sb(es, [128, 64, TAU, 32], BF16, "Wfar"),
        "Knear": k.sb(es, [128, 4, 16, 128], BF16, "Knear"),
        "LRm": k.sb(es, [64, 2, 2, 32, 2], F32, "LRm"),
        "LIm": k.sb(es, [64, 2, 2, 32, 2], F32, "LIm"),
        "bdmask": k.sb(es, [128, 128], F32, "bdmask"),
        "pmask": k.sb(es, [128, 2], F32, "pmask"),
    }
    k.S.dma("sp", T["bdmask"][:, :], d["cst_bdmask"], w=["bdmask"])
    k.S.dma("sp", T["pmask"][:, :], d["cst_pmask"], w=["pmask"])
    return T


def s5_params(d):
    return {"lam_re": d["ssm_lambda_re"][0], "lam_im": d["ssm_lambda_im"][0], "log_dt": d["ssm_log_dt"][0],
            "b_re": d["ssm_b_re"][0], "b_im": d["ssm_b_im"][0], "c_re": d["ssm_c_re"][0], "c_im": d["ssm_c_im"][0],
            "ssm_d": d["ssm_d"][0], "w_glu": d["ssm_w_glu"][0], "b_glu": d["ssm_b_glu"][0]}


def build_program(debug=False):
    nc = bass.Bass("TRN2", target_bir_lowering=False)
    d = {n: nc.dram_tensor(n, list(s), F32, kind="ExternalInput").ap() for n, s in IN_SPECS}
    skind = "ExternalOutput" if debug else "Internal"

    def scratch(name, shape, dt):
        return nc.dram_tensor(name, list(shape), dt, kind=skind).ap()
    out_d = nc.dram_tensor("out", [NSEQ, SEQ, D], F32, kind="ExternalOutput").ap()
    grow = scratch("s_grow", [2, 2, 3, D], F32)
    AT = scratch("s_AT", [NSEQ, 128, 4, A_LEN], BF16)
    UT = scratch("s_UT", [NSEQ, 128, 4, NTOK], BF16)
    CT = scratch("s_CT", [NSEQ, 128, 4, NTOK], BF16)
    ST = scratch("s_ST", [NSEQ, 128, 4, NTOK], BF16)
    SIN = scratch("s_SIN", [NSEQ, 2, 128, NCH, 32], BF16)
    X1 = scratch("s_X1", [NSEQ, NTOK, D], F32)
    X2 = scratch("s_X2", [NSEQ, NTOK, D], F32)
    X3 = scratch("s_X3", [NSEQ, SEQ, D], F32)
    KT = scratch("s_KT", [NSEQ, 128, 2, NTOK], BF16)
    VV = scratch("s_V", [NSEQ, NTOK, 256], BF16)
    k = K(nc)
    phase_mod(k, d["c"], d["c_ctx"], d["w_mod"], d["b_mod"], grow)
    phase_l0a(k, d["x"], d["ctx"], d["w_in_ab"][0], AT, UT)
    phase_conv(k, AT, CT, d["conv_w"][0], d["conv_b"][0], d["conv_ln_g"][0], d["conv_ln_b"][0])
    with ExitStack() as es:
        T = s5_tables(k, es, d)
        p = s5_params(d)
        phase_s5prep(k, T, p)
        phase_s5a(k, T, UT, SIN)
        phase_s5b(k, T, UT, SIN, ST, p)
    blocks = []
    for b in range(NSEQ):
        for (t0, Tn, isctx) in seq_blocks(512):
            blocks.append(([CT[b][:, :, t0:t0 + Tn], ST[b][:, :, t0:t0 + Tn]],
                           d["ctx"][b, t0:t0 + Tn, :] if isctx else d["x"][b, t0 - CTX:t0 - CTX + Tn, :],
                           X1[b][t0:t0 + Tn, :], 2 if isctx else b))
    phase_outproj(k, 0, None, d["w_out_ab"][0], d["ln_g"][0, 0], d["ln_b"][0, 0], grow, blocks)
    fb = []
    for b in range(NSEQ):
        for (t0, Tn, isctx) in seq_blocks(256):
            fb.append((X1[b][t0:t0 + Tn, :], X2[b][t0:t0 + Tn, :], 2 if isctx else b))
    phase_ffn(k, 0, fb, d["w_ffn_in"][0], d["w_ffn_out"][0], d["ln_g"][0, 1], d["ln_b"][0, 1], grow)
    phase_l1a(k, X2, d["w_in_c"][0], d["k_norm_g"][0], KT, VV, d["cst_rope"])
    phase_l1b(k, X2, d["w_in_c"][0], d["q_norm_g"][0], d["w_out_c"][0], KT, VV, d["cst_rope"],
              d["ln_g"][1, 0], d["ln_b"][1, 0], grow, X3)
    fb = []
    for b in range(NSEQ):
        for t0 in range(0, SEQ, 256):
            fb.append((X3[b][t0:t0 + 256, :], out_d[b][t0:t0 + 256, :], b))
    phase_ffn(k, 1, fb, d["w_ffn_in"][1], d["w_ffn_out"][1], d["ln_g"][1, 1], d["ln_b"][1, 1], grow)
    k.S.emit(final_waits=["st"])
    return nc


_PROGRAM = None


def kernel(**inputs):
    global _PROGRAM
    if _PROGRAM is None:
        _PROGRAM = build_program()
    nc = _PROGRAM
    cst = host_consts()
    in_maps = []
    for core in range(8):
        m = {}
        sl = slice(core * NSEQ, (core + 1) * NSEQ)
        for n, _ in IN_SPECS:
            if n.startswith("cst_"):
                m[n] = cst[n]
            elif n in ("x", "c", "ctx"):
                m[n] = np.ascontiguousarray(np.asarray(inputs[n], dtype=np.float32)[sl])
            else:
                m[n] = np.ascontiguousarray(np.asarray(inputs[n], dtype=np.float32))
        in_maps.append(m)
    res = run_bass_kernel_spmd(nc, in_maps, core_ids=list(range(8)))
    return np.concatenate([np.asarray(r["out"], dtype=np.float32) for r in res.results], axis=0)
```
